# Optimizing a Trainium2 kernel written in Bass

```python
import math
import jax, jax.numpy as jnp
from jax import lax
import numpy as np

D_MODEL = 2048
BATCH = 2
SEQ = 8192
DEPTH = 2

A_HEADS = 8
A_KV_HEADS = 2
A_HEAD_DIM = 64
A_WIDTH = A_HEADS * A_HEAD_DIM
A_KV_WIDTH = A_KV_HEADS * A_HEAD_DIM
WINDOW = 128
A_BLOCK = 128
ROPE_THETA = 10000.0
R_WIDTH = 1024
R_BLOCKS = 8
R_BLOCK_DIM = R_WIDTH // R_BLOCKS
R_C = 8.0
CONV_WIDTH = 4
G_HEADS = 4
G_HEAD_DIM = 128
G_WIDTH = G_HEADS * G_HEAD_DIM
G_CHUNK = 64
MIX_WIDTH = A_WIDTH + R_WIDTH + G_WIDTH
IN_SIZES = (A_WIDTH, A_KV_WIDTH, A_KV_WIDTH, A_WIDTH, R_WIDTH, R_WIDTH,
            G_WIDTH, G_WIDTH, G_WIDTH, G_WIDTH, G_HEADS, G_HEADS)
N_IN = sum(IN_SIZES)
DEEPNORM_ALPHA = (2 * DEPTH) ** 0.25
DEEPNORM_BETA = (8 * DEPTH) ** -0.25
LN_EPS = 1e-5
RMS_EPS = 1e-6

kernel_name = "hybrid_swa_rglru_gdn_deepnorm"


def layer_norm(x, g, b):
    xf = x.astype(jnp.float32)
    mu = jnp.mean(xf, -1, keepdims=True)
    var = jnp.mean(jnp.square(xf - mu), -1, keepdims=True)
    return ((xf - mu) * lax.rsqrt(var + LN_EPS) * g.astype(jnp.float32) + b.astype(jnp.float32)).astype(x.dtype)


def rope_tables(seq, dim):
    inv = 1.0 / (ROPE_THETA ** (jnp.arange(0, dim, 2, dtype=jnp.float32) / dim))
    ang = jnp.arange(seq, dtype=jnp.float32)[:, None] * inv[None, :]
    return jnp.cos(ang), jnp.sin(ang)


def apply_rope(x, cos, sin):
    xf = x.astype(jnp.float32)
    x1, x2 = jnp.split(xf, 2, axis=-1)
    c = cos[None, :, None, :]
    s = sin[None, :, None, :]
    return jnp.concatenate([x1 * c - x2 * s, x2 * c + x1 * s], axis=-1).astype(x.dtype)


def causal_depthwise_conv(x, w):
    return lax.conv_general_dilated(
        x, w[:, None, :].astype(x.dtype), window_strides=(1,),
        padding=[(CONV_WIDTH - 1, 0)], dimension_numbers=("NWC", "WIO", "NWC"),
        feature_group_count=x.shape[-1])


def sliding_window_attention(q, k, v, sinks):
    B, S, _, D = q.shape
    nb = S // A_BLOCK
    grp = A_HEADS // A_KV_HEADS
    qb = q.reshape(B, nb, A_BLOCK, A_KV_HEADS, grp, D)
    pad = ((0, 0), (A_BLOCK, 0), (0, 0), (0, 0))
    kp = jnp.pad(k, pad).reshape(B, nb + 1, A_BLOCK, A_KV_HEADS, D)
    vp = jnp.pad(v, pad).reshape(B, nb + 1, A_BLOCK, A_KV_HEADS, D)
    kw = jnp.concatenate([kp[:, :-1], kp[:, 1:]], axis=2)
    vw = jnp.concatenate([vp[:, :-1], vp[:, 1:]], axis=2)
    scores = jnp.einsum("bnqhgd,bnkhd->bhgnqk", qb, kw).astype(jnp.float32) * (D ** -0.5)
    i = jnp.arange(A_BLOCK)[:, None]
    j = jnp.arange(2 * A_BLOCK)[None, :]
    diff = i - j + A_BLOCK
    band = (diff >= 0) & (diff < WINDOW)
    kpos = (jnp.arange(nb)[:, None, None] - 1) * A_BLOCK + j[None]
    mask = band[None] & (kpos >= 0)
    scores = jnp.where(mask, scores, -jnp.inf)
    sink = jnp.broadcast_to(
        sinks.astype(jnp.float32).reshape(1, A_KV_HEADS, grp, 1, 1, 1),
        scores.shape[:-1] + (1,))
    probs = jax.nn.softmax(jnp.concatenate([scores, sink], axis=-1), axis=-1)[..., :-1]
    out = jnp.einsum("bhgnqk,bnkhd->bnqhgd", probs.astype(v.dtype), vw)
    return out.reshape(B, S, A_HEADS * D)


def rg_lru(x, w_a, b_a, w_x, b_x, lam):
    B, S, _ = x.shape
    xb = x.reshape(B, S, R_BLOCKS, R_BLOCK_DIM)
    r = jax.nn.sigmoid(jnp.einsum("bsnc,ncd->bsnd", xb, w_a).reshape(B, S, R_WIDTH) + b_a)
    ig = jax.nn.sigmoid(jnp.einsum("bsnc,ncd->bsnd", xb, w_x).reshape(B, S, R_WIDTH) + b_x)
    log_a = -R_C * r.astype(jnp.float32) * jax.nn.softplus(-lam.astype(jnp.float32))
    a = jnp.exp(log_a)
    u = jnp.sqrt(-jnp.expm1(2.0 * log_a)) * (ig * x).astype(jnp.float32)

    def combine(left, right):
        a1, b1 = left
        a2, b2 = right
        return a1 * a2, a2 * b1 + b2

    _, h = lax.associative_scan(combine, (a, u), axis=1)
    return h.astype(x.dtype)


def gated_delta_chunked(q, k, v, g, beta):
    B, S, H, Dk = q.shape
    Dv = v.shape[-1]
    N = S // G_CHUNK
    C = G_CHUNK

    def chunks(t):
        return t.reshape(B, N, C, H, -1).transpose(0, 3, 1, 2, 4)

    q, k, v = chunks(q), chunks(k), chunks(v)
    g = jnp.cumsum(g.reshape(B, N, C, H).transpose(0, 3, 1, 2), axis=-1)
    beta = beta.reshape(B, N, C, H).transpose(0, 3, 1, 2)
    tril = jnp.tril(jnp.ones((C, C), dtype=bool))
    strict = jnp.tril(jnp.ones((C, C), dtype=bool), -1)
    decay = jnp.exp(jnp.where(tril, g[..., :, None] - g[..., None, :], -jnp.inf))
    kb = k * beta[..., None]
    vb = v * beta[..., None]
    m = jnp.where(strict, jnp.einsum("bhncd,bhnjd->bhncj", kb, k) * decay, 0.0)
    lhs = jnp.eye(C, dtype=jnp.float32) + m
    u = lax.linalg.triangular_solve(lhs, vb, left_side=True, lower=True, unit_diagonal=True)
    w = lax.linalg.triangular_solve(lhs, kb * jnp.exp(g)[..., None], left_side=True,
                                    lower=True, unit_diagonal=True)
    qk = jnp.where(tril, jnp.einsum("bhncd,bhnjd->bhncj", q, k) * decay, 0.0)
    q_dec = q * jnp.exp(g)[..., None]
    k_dec = k * jnp.exp(g[..., -1:] - g)[..., None]
    g_last = jnp.exp(g[..., -1])

    def step(state, inp):
        qk_i, qd_i, kd_i, u_i, w_i, gl_i = inp
        v_new = u_i - jnp.einsum("bhcd,bhde->bhce", w_i, state)
        o = jnp.einsum("bhcd,bhde->bhce", qd_i, state) + jnp.einsum("bhcj,bhje->bhce", qk_i, v_new)
        state = state * gl_i[..., None, None] + jnp.einsum("bhcd,bhce->bhde", kd_i, v_new)
        return state, o

    xs = tuple(jnp.moveaxis(t, 2, 0) for t in (qk, q_dec, k_dec, u, w, g_last))
    state0 = jnp.zeros((B, H, Dk, Dv), jnp.float32)
    _, o = lax.scan(step, state0, xs)
    return o.transpose(1, 0, 3, 2, 4).reshape(B, S, H, Dv)


def l2norm(x):
    return x * lax.rsqrt(jnp.sum(x * x, -1, keepdims=True) + RMS_EPS)


def hybrid_layer(x, cos, sin, w_in, sinks, r_conv_w, r_conv_b, r_wa, r_ba, r_wx, r_bx, r_lam,
                 g_conv_w, g_a_log, g_dt_bias, g_norm_w, w_out, ln_g, ln_b):
    B, S, _ = x.shape
    proj = x @ w_in
    points = [int(p) for p in np.cumsum(IN_SIZES)[:-1]]
    aq, ak, av, az, rx, rz, gq, gk, gv, gz, gb, ga = jnp.split(proj, points, axis=-1)

    q = apply_rope(aq.reshape(B, S, A_HEADS, A_HEAD_DIM), cos, sin)
    k = apply_rope(ak.reshape(B, S, A_KV_HEADS, A_HEAD_DIM), cos, sin)
    v = av.reshape(B, S, A_KV_HEADS, A_HEAD_DIM)
    y_a = sliding_window_attention(q, k, v, sinks) * jax.nn.silu(az)

    xr = causal_depthwise_conv(rx, r_conv_w) + r_conv_b
    y_r = rg_lru(xr, r_wa, r_ba, r_wx, r_bx, r_lam) * jax.nn.silu(rz)

    qkv = jax.nn.silu(causal_depthwise_conv(jnp.concatenate([gq, gk, gv], axis=-1), g_conv_w))
    cq, ck, cv = jnp.split(qkv.astype(jnp.float32), 3, axis=-1)
    cq = l2norm(cq.reshape(B, S, G_HEADS, G_HEAD_DIM)) * (G_HEAD_DIM ** -0.5)
    ck = l2norm(ck.reshape(B, S, G_HEADS, G_HEAD_DIM))
    cv = cv.reshape(B, S, G_HEADS, G_HEAD_DIM)
    beta = jax.nn.sigmoid(gb.astype(jnp.float32))
    g = -jnp.exp(g_a_log.astype(jnp.float32)) * jax.nn.softplus(
        ga.astype(jnp.float32) + g_dt_bias.astype(jnp.float32))
    o = gated_delta_chunked(cq, ck, cv, g, beta)
    o = o * lax.rsqrt(jnp.mean(o * o, -1, keepdims=True) + RMS_EPS) * g_norm_w.astype(jnp.float32)
    y_g = o.reshape(B, S, G_WIDTH).astype(x.dtype) * jax.nn.silu(gz)

    y = jnp.concatenate([y_a, y_r, y_g], axis=-1) @ w_out
    return layer_norm(DEEPNORM_ALPHA * x + y, ln_g, ln_b)


def setup_inputs(seed: int = 0) -> dict:
    key = jax.random.key(seed)
    ks = jax.random.split(key, 20)
    f32 = jnp.float32
    x = jax.random.normal(ks[0], (BATCH, SEQ, D_MODEL), f32)
    w_in = jax.random.normal(ks[1], (DEPTH, D_MODEL, N_IN), f32) * D_MODEL ** -0.5
    sinks = jax.random.normal(ks[2], (DEPTH, A_HEADS), f32)
    r_conv_w = jax.random.normal(ks[3], (DEPTH, CONV_WIDTH, R_WIDTH), f32) * CONV_WIDTH ** -0.5
    r_conv_b = jax.random.normal(ks[4], (DEPTH, R_WIDTH), f32) * 0.01
    r_wa = jax.random.normal(ks[5], (DEPTH, R_BLOCKS, R_BLOCK_DIM, R_BLOCK_DIM), f32) * R_BLOCK_DIM ** -0.5
    r_ba = jax.random.normal(ks[6], (DEPTH, R_WIDTH), f32) * 0.01
    r_wx = jax.random.normal(ks[7], (DEPTH, R_BLOCKS, R_BLOCK_DIM, R_BLOCK_DIM), f32) * R_BLOCK_DIM ** -0.5
    r_bx = jax.random.normal(ks[8], (DEPTH, R_WIDTH), f32) * 0.01
    a_c = jax.random.uniform(ks[9], (DEPTH, R_WIDTH), f32, minval=0.9, maxval=0.999)
    a0 = a_c ** (1.0 / R_C)
    r_lam = jnp.log(a0) - jnp.log1p(-a0)
    g_conv_w = jax.random.normal(ks[10], (DEPTH, CONV_WIDTH, 3 * G_WIDTH), f32) * CONV_WIDTH ** -0.5
    g_a_log = jnp.log(jax.random.uniform(ks[11], (DEPTH, G_HEADS), f32, minval=1.0, maxval=16.0))
    dt = jnp.exp(jax.random.uniform(ks[12], (DEPTH, G_HEADS), f32,
                                    minval=math.log(1e-3), maxval=math.log(1e-1)))
    g_dt_bias = dt + jnp.log(-jnp.expm1(-dt))
    g_norm_w = 1.0 + 0.01 * jax.random.normal(ks[13], (DEPTH, G_HEAD_DIM), f32)
    w_out = jax.random.normal(ks[14], (DEPTH, MIX_WIDTH, D_MODEL), f32) * (MIX_WIDTH ** -0.5) * DEEPNORM_BETA
    ln_g = 1.0 + 0.01 * jax.random.normal(ks[15], (DEPTH, D_MODEL), f32)
    ln_b = 0.01 * jax.random.normal(ks[16], (DEPTH, D_MODEL), f32)
    return {"x": x, "w_in": w_in, "sinks": sinks, "r_conv_w": r_conv_w, "r_conv_b": r_conv_b,
            "r_wa": r_wa, "r_ba": r_ba, "r_wx": r_wx, "r_bx": r_bx, "r_lam": r_lam,
            "g_conv_w": g_conv_w, "g_a_log": g_a_log, "g_dt_bias": g_dt_bias, "g_norm_w": g_norm_w,
            "w_out": w_out, "ln_g": ln_g, "ln_b": ln_b}


def reference(x, w_in, sinks, r_conv_w, r_conv_b, r_wa, r_ba, r_wx, r_bx, r_lam,
              g_conv_w, g_a_log, g_dt_bias, g_norm_w, w_out, ln_g, ln_b):
    cos, sin = rope_tables(x.shape[1], A_HEAD_DIM)
    for l in range(DEPTH):
        x = hybrid_layer(x, cos, sin, w_in[l], sinks[l], r_conv_w[l], r_conv_b[l], r_wa[l], r_ba[l],
                         r_wx[l], r_bx[l], r_lam[l], g_conv_w[l], g_a_log[l], g_dt_bias[l],
                         g_norm_w[l], w_out[l], ln_g[l], ln_b[l])
    return x
```

```python
import math
import numpy as np
import concourse.bass as bass
import concourse.mybir as mybir
from contextlib import ExitStack
from concourse.bass_utils import run_bass_kernel_spmd

F32 = mybir.dt.float32
BF16 = mybir.dt.bfloat16
ALU = mybir.AluOpType
AF = mybir.ActivationFunctionType

ENGS = ("pe", "act", "dve", "pool", "sp")
SAME_ENG_SKIP = {"act": 3, "dve": 3, "pool": 10 ** 9, "sp": 3}
WINDOW = 64


class View:
    __slots__ = ("tile", "p0", "p1", "f0", "f1", "_ap")

    def __init__(self, tile, p0, p1, f0, f1):
        self.tile, self.p0, self.p1, self.f0, self.f1 = tile, p0, p1, f0, f1
        self._ap = None

    @property
    def ap(self):
        if self._ap is None:
            self._ap = self.tile.h[self.p0:self.p1, self.f0:self.f1]
        return self._ap

    def sub(self, f0, f1, p0=None, p1=None):
        np0 = self.p0 if p0 is None else self.p0 + p0
        np1 = self.p1 if p1 is None else self.p0 + p1
        return View(self.tile, np0, np1, self.f0 + f0, self.f0 + f1)

    def r3(self, a):
        return self.ap.rearrange("p (a b) -> p a b", a=a)


class Tile:
    def __init__(self, name, h, P, F, space):
        self.name, self.h, self.P, self.F, self.space = name, h, P, F, space
        self.recs = []

    def v(self, f0=0, f1=None, p0=0, p1=None):
        return View(self, p0, self.P if p1 is None else p1, f0, self.F if f1 is None else f1)


class DramRegion:
    pass


class Op:
    __slots__ = ("id", "eng", "fn", "deps", "cost", "sig", "dma_sem", "dma_cnt", "start", "end", "name")


class Prog:
    def __init__(self, nc, stack):
        self.nc, self.stack = nc, stack
        self.ops = []
        self.tiles = {}
        self.dma_sems = {}

    def sbuf(self, name, P, F, dt):
        h = self.stack.enter_context(self.nc.sbuf_tensor(name, [P, F], dt))
        t = Tile(name, h, P, F, "sb")
        self.tiles[name] = t
        return t

    def psum(self, name, P, F, dt=F32):
        h = self.stack.enter_context(self.nc.psum_tensor(name, [P, F], dt))
        t = Tile(name, h, P, F, "ps")
        self.tiles[name] = t
        return t

    def dsem(self, name):
        if name not in self.dma_sems:
            h = self.stack.enter_context(self.nc.semaphore("d_" + name))
            self.dma_sems[name] = [h, 0]
        return name

    def _deps(self, op, reads, writes):
        deps = set()
        ps = [v for v in list(reads) + list(writes) if v.tile.space == "ps"]
        reads = [v for v in reads if v.tile.space != "ps"]
        writes = [v for v in writes if v.tile.space != "ps"]
        seen = set()
        for v in ps:
            if v.tile.name not in seen:
                seen.add(v.tile.name)
                writes.append(v.tile.v())
        for v in reads:
            recs = v.tile.recs
            for r in recs:
                if r[5] and r[0] < v.p1 and v.p0 < r[1] and r[2] < v.f1 and v.f0 < r[3]:
                    deps.add(r[4])
            recs.append([v.p0, v.p1, v.f0, v.f1, op.id, False])
        for v in writes:
            recs = v.tile.recs
            keep = []
            for r in recs:
                if r[0] < v.p1 and v.p0 < r[1] and r[2] < v.f1 and v.f0 < r[3]:
                    if r[4] != op.id:
                        deps.add(r[4])
                    if r[0] >= v.p0 and r[1] <= v.p1 and r[2] >= v.f0 and r[3] <= v.f1:
                        continue
                keep.append(r)
            keep.append([v.p0, v.p1, v.f0, v.f1, op.id, True])
            v.tile.recs = keep
        deps.discard(op.id)
        return deps

    def add(self, eng, fn, reads=(), writes=(), cost=0.3, dma=None, name=""):
        op = Op()
        op.id = len(self.ops)
        op.eng, op.fn, op.cost, op.name = eng, fn, cost, name
        op.sig = False
        op.dma_sem = dma
        op.dma_cnt = 0
        op.deps = self._deps(op, reads, writes)
        self.ops.append(op)
        return op

    def schedule(self, window=24):
        ops = self.ops
        per = {e: [o for o in ops if o.eng == e] for e in ENGS}
        pos = {e: 0 for e in ENGS}
        done = [False] * len(ops)
        endt = [0.0] * len(ops)
        free = {e: 0.0 for e in ENGS}
        order = {e: [] for e in ENGS}
        pend = {e: list(per[e]) for e in ENGS}
        n_left = len(ops)
        LAT = 0.15
        while n_left:
            best = None
            for e in ENGS:
                lst = pend[e]
                if not lst:
                    continue
                lim = 1 if e == "pe" else min(window, len(lst))
                for i in range(lim):
                    o = lst[i]
                    ok = True
                    rt = 0.0
                    for d in o.deps:
                        if not done[d]:
                            ok = False
                            break
                        t = endt[d] + (LAT if ops[d].eng != e or e != "pe" else 0.0)
                        if t > rt:
                            rt = t
                    if not ok:
                        continue
                    st = max(rt, free[e])
                    key = (st, o.id)
                    if best is None or key < best[0]:
                        best = (key, e, i, o)
                    if st <= free[e]:
                        break
            assert best is not None, "scheduler deadlock"
            (st, _), e, i, o = best
            pend[e].pop(i)
            o.start = st
            dur = o.cost
            if o.dma_sem is not None:
                free[e] = st + 0.1
                endt[o.id] = st + 2.0 + dur
            else:
                free[e] = st + dur
                endt[o.id] = st + dur
            o.end = endt[o.id]
            done[o.id] = True
            order[e].append(o)
            n_left -= 1
        self.order = order
        self.est = max(endt) if endt else 0.0
        return order

    def emit(self):
        nc = self.nc
        ops = self.ops
        order = self.order
        idx_in_eng = {}
        for e in ENGS:
            for k, o in enumerate(order[e]):
                idx_in_eng[o.id] = k
        for o in ops:
            for d in o.deps:
                do = ops[d]
                if do.dma_sem is not None:
                    continue
                if do.eng == o.eng:
                    if o.eng == "pe":
                        continue
                    if idx_in_eng[o.id] - idx_in_eng[d] > SAME_ENG_SKIP.get(o.eng, 3):
                        continue
                do.sig = True
        sems = {e: self.stack.enter_context(nc.semaphore("s_" + e)) for e in ENGS}
        cnt = {}
        for e in ENGS:
            c = 0
            for o in order[e]:
                if o.dma_sem is not None:
                    s = self.dma_sems[o.dma_sem]
                    s[1] += 16
                    o.dma_cnt = s[1]
                elif o.sig:
                    c += 1
                    cnt[o.id] = c
        sem_eng = {}
        for o in ops:
            if o.dma_sem is not None:
                assert sem_eng.setdefault(o.dma_sem, o.eng) == o.eng, "dma sem used from two engines"
        self.n_wait = 0

        def emit_engine(e, eh):
            waited = {}
            for o in order[e]:
                need = {}
                for d in o.deps:
                    do = ops[d]
                    if do.dma_sem is not None:
                        key = ("d", do.dma_sem)
                        val = do.dma_cnt
                    else:
                        if not do.sig:
                            continue
                        if do.eng == e and (e == "pe" or idx_in_eng[o.id] - idx_in_eng[d] > SAME_ENG_SKIP.get(e, 3)):
                            continue
                        key = ("e", do.eng)
                        val = cnt[d]
                    if waited.get(key, 0) >= val:
                        continue
                    if need.get(key, 0) < val:
                        need[key] = val
                for key, val in need.items():
                    waited[key] = val
                    sh = self.dma_sems[key[1]][0] if key[0] == "d" else sems[key[1]]
                    eh.wait_ge(sh, val)
                    self.n_wait += 1
                ins = o.fn(eh)
                if o.dma_sem is not None:
                    ins.then_inc(self.dma_sems[o.dma_sem][0], 16)
                elif o.sig:
                    ins.then_inc(sems[e], 1)

        with nc.Block() as block:
            @block.tensor
            def _(eh):
                emit_engine("pe", eh)

            @block.scalar
            def _(eh):
                emit_engine("act", eh)

            @block.vector
            def _(eh):
                emit_engine("dve", eh)

            @block.gpsimd
            def _(eh):
                emit_engine("pool", eh)

            @block.sync
            def _(eh):
                emit_engine("sp", eh)
                for nm, (sh, c) in self.dma_sems.items():
                    if c > 0:
                        eh.wait_ge(sh, c)


T = 512
NS = 5
NCH_IN = 43
NCH = 59
ALPHA = (2 * 2) ** 0.25
LN_EPS = 1e-5
RMS_EPS = 1e-6
R_C = 8.0
C_RCW, C_RCB, C_BA, C_BX, C_LAM, C_GCW, C_LNG, C_LNB, C_GNW, C_SINK = 0, 32, 40, 48, 56, 64, 112, 128, 144, 145
NCOL_L = 152


def mmc(N, fp32=False):
    return (0.03 + N * 0.00045) * (4 if fp32 else 1)


def actc(F):
    return 0.22 + F / 1200.0


def dvec(F):
    return 0.13 + F / 960.0


def poolc(F):
    return 0.15 + F / 500.0


DBG = False
GSTOP = 0


class StopBuild(Exception):
    pass


def build_program(SEQ, NL=2, en=(1, 1, 1)):
    NG = SEQ // T
    nc = bass.Bass("TRN2", target_bir_lowering=False)
    xT = nc.dram_tensor("xT", [16, 128, SEQ], F32, kind="ExternalInput").ap()
    wall = nc.dram_tensor("wall", [NL, NCH, 128, 16 * 128], F32, kind="ExternalInput").ap()
    cosT = nc.dram_tensor("cosT", [128, SEQ], F32, kind="ExternalInput").ap()
    sinT = nc.dram_tensor("sinT", [128, SEQ], F32, kind="ExternalInput").ap()
    wab = nc.dram_tensor("wab", [NL, 2, 128, 8 * 128], F32, kind="ExternalInput").ap()
    colsD = nc.dram_tensor("cols", [128, NL * NCOL_L], F32, kind="ExternalInput").ap()
    sm4D = nc.dram_tensor("sm4", [4, NL * 2], F32, kind="ExternalInput").ap()
    cst = nc.dram_tensor("cst", [128, 9 * 128], F32, kind="ExternalInput").ap()
    cst4 = nc.dram_tensor("cst4", [4, 1024], F32, kind="ExternalInput").ap()
    outT = nc.dram_tensor("outT", [16, 128, SEQ], F32, kind="ExternalOutput").ap()
    wbf = nc.dram_tensor("wbf", [NL, NCH, 128, 16 * 128], BF16).ap()

    st = ExitStack()
    P = Prog(nc, st)
    A = P.add
    XF = P.sbuf("XF", 128, 16 * T, F32)
    XB = P.sbuf("XB", 128, 16 * T, BF16)
    YT = P.sbuf("YT", 128, 16 * T, BF16)
    WR = [P.sbuf(f"WR{i}", 128, 2048, BF16) for i in range(NS)]
    for i in range(NS):
        P.dsem(f"w{i}")
    for nm_ in ["c0", "c1", "c2", "c3", "cp", "t0", "t1"] + [f"x{i}" for i in range(16)] + [f"o{i}" for i in range(16)]:
        P.dsem(nm_)
    CST = P.sbuf("CST", 128, 9 * 128, F32)
    CST4 = P.sbuf("CST4", 4, 1024, F32)
    COLS = P.sbuf("COLS", 128, NL * NCOL_L, F32)
    SM4 = P.sbuf("SM4", 4, NL * 2, F32)
    WAB = P.sbuf("WAB", 128, NL * 2 * 1024, BF16)
    IDB = P.sbuf("IDB", 128, 128, BF16)
    ONB = P.sbuf("ONB", 128, 128, BF16)
    ONAB = P.sbuf("ONAB", 128, 256, BF16)
    MSK = P.sbuf("MSK", 128, 1024, BF16)
    NMK = P.sbuf("NMK", 128, 512, BF16)
    DER = P.sbuf("DER", 128, NL * 40, F32)
    NEGA = P.sbuf("NEGA", 4, NL * 2, F32)
    COS = P.sbuf("COS", 128, T, F32)
    SIN = P.sbuf("SIN", 128, T, F32)
    S32 = P.sbuf("S32", 128, 10580, F32)
    S16 = P.sbuf("S16", 128, 19456, BF16)
    KB = [[P.sbuf(f"KB{l}_{i}", 128, 128 + T, BF16) for i in range(2)] for l in range(NL)]
    VAB = [P.sbuf(f"VAB{l}", 128, 5 * 256, BF16) for l in range(NL)]
    RXH = [P.sbuf(f"RXH{l}", 128, 24, F32) for l in range(NL)]
    HST = [P.sbuf(f"HST{l}", 128, 8, F32) for l in range(NL)]
    GXH = [P.sbuf(f"GXH{l}", 128, 36, F32) for l in range(NL)]
    SST = [P.sbuf(f"SST{l}", 128, 512, F32) for l in range(NL)]
    SSB = [P.sbuf(f"SSB{l}", 128, 512, BF16) for l in range(NL)]
    SMALL = P.sbuf("SMALL", 128, 256, F32)
    WBF = Tile("WBF", None, 1, NL * NCH, "dram")
    CVR = Tile("CVR", None, 1, 8, "dram")
    for i in range(8):
        P.dsem(f"cv{i}")
    banks = [P.psum(f"B{i}", 128, 512) for i in range(8)]
    bankb = {}
    rr = {"proj": 0, "misc": 0}

    def pbank():
        rr["proj"] += 1
        return banks[rr["proj"] % 3]

    def mbank():
        rr["misc"] += 1
        return banks[3 + rr["misc"] % 5]

    IDENT = CST.v(0, 128); PERM = CST.v(128, 256); ONESF = CST.v(256, 384); OFFD = CST.v(384, 512)
    LASTSEL = CST.v(512, 640)
    SEL = CST4.v(0, 512); RST = CST4.v(512, 1024)

    A("sp", lambda e: e.dma_start(out=CST.v().ap, in_=cst), writes=[CST.v()], dma="c0", cost=2)
    A("sp", lambda e: e.dma_start(out=CST4.v().ap, in_=cst4), writes=[CST4.v()], dma="c1", cost=2)
    A("sp", lambda e: e.dma_start(out=COLS.v().ap, in_=colsD), writes=[COLS.v()], dma="c2", cost=2)
    A("sp", lambda e: e.dma_start(out=SM4.v().ap, in_=sm4D), writes=[SM4.v()], dma="c3", cost=2)
    A("pool", lambda e: e.dma_start(out=WAB.v().ap.rearrange("p (a f) -> p a f", a=NL * 2),
                                    in_=wab.rearrange("l m p f -> p (l m) f")), writes=[WAB.v()], dma="cp", cost=4)
    A("dve", lambda e: e.tensor_copy(out=IDB.v().ap, in_=IDENT.ap), reads=[IDENT], writes=[IDB.v()])
    A("dve", lambda e: e.tensor_copy(out=ONB.v().ap, in_=ONESF.ap), reads=[ONESF], writes=[ONB.v()])
    A("dve", lambda e: e.memset(ONAB.v().ap, 0.0), writes=[ONAB.v()])
    A("dve", lambda e: e.memset(ONAB.v(0, 64).ap, 1.0), writes=[ONAB.v(0, 64)])
    A("dve", lambda e: e.memset(ONAB.v(192, 256).ap, 1.0), writes=[ONAB.v(192, 256)])
    for i in range(2):
        A("dve", lambda e, i=i: e.tensor_copy(out=MSK.v(i * 256, i * 256 + 256).ap, in_=CST.v(640, 896).ap), reads=[CST.v(640, 896)], writes=[MSK.v(i * 256, i * 256 + 256)])
        A("dve", lambda e, i=i: e.tensor_copy(out=MSK.v(512 + i * 256 + 128, 512 + i * 256 + 256).ap, in_=CST.v(768, 896).ap), reads=[CST.v(768, 896)], writes=[MSK.v(512 + i * 256 + 128, 512 + i * 256 + 256)])
        A("dve", lambda e, i=i: e.memset(MSK.v(512 + i * 256, 512 + i * 256 + 128).ap, 0.0), writes=[MSK.v(512 + i * 256, 512 + i * 256 + 128)])
    for i in range(4):
        A("dve", lambda e, i=i: e.tensor_copy(out=NMK.v(i * 128, i * 128 + 128).ap, in_=CST.v(896, 1024).ap), reads=[CST.v(896, 1024)], writes=[NMK.v(i * 128, i * 128 + 128)])
    for l in range(NL):
        cb = l * NCOL_L
        d0 = l * 40
        lam = COLS.v(cb + C_LAM, cb + C_LAM + 8)
        cn = DER.v(d0, d0 + 8); cn2 = DER.v(d0 + 8, d0 + 16); es = DER.v(d0 + 16, d0 + 20)
        A("act", lambda e, lam=lam, cn=cn: e.activation(out=cn.ap, in_=lam.ap, func=AF.Exp, scale=-1.0), reads=[lam], writes=[cn])
        A("act", lambda e, cn=cn: e.activation(out=cn.ap, in_=cn.ap, func=AF.Ln, bias=1.0), reads=[cn], writes=[cn])
        A("dve", lambda e, cn=cn, cn2=cn2: e.tensor_scalar_mul(out=cn2.ap, in0=cn.ap, scalar1=-R_C), reads=[cn], writes=[cn2])
        A("dve", lambda e, cn=cn: e.tensor_scalar_mul(out=cn.ap, in0=cn.ap, scalar1=-0.5 * R_C), reads=[cn, cn2], writes=[cn])
        hb = DER.v(d0 + 20, d0 + 36)
        bab = COLS.v(cb + C_BA, cb + C_BA + 16)
        A("dve", lambda e, hb=hb, bab=bab: e.tensor_scalar_mul(out=hb.ap, in0=bab.ap, scalar1=0.5), reads=[bab], writes=[hb])
        sk = COLS.v(cb + C_SINK, cb + C_SINK + 4)
        A("act", lambda e, sk=sk, es=es: e.activation(out=es.ap, in_=sk.ap, func=AF.Exp), reads=[sk], writes=[es])
        na = NEGA.v(l * 2, l * 2 + 1)
        A("act", lambda e, na=na, l=l: e.activation(out=na.ap, in_=SM4.v(l * 2, l * 2 + 1).ap, func=AF.Exp), reads=[SM4.v()], writes=[na])
        A("dve", lambda e, na=na: e.tensor_scalar_mul(out=na.ap, in0=na.ap, scalar1=-1.0), reads=[na], writes=[na])
        for tl in (KB[l][0], KB[l][1], VAB[l]):
            A("pool", lambda e, tl=tl: e.memset(tl.v().ap, 0.0), writes=[tl.v()])
        for tl in (RXH[l], HST[l], GXH[l], SST[l]):
            A("pool", lambda e, tl=tl: e.memset(tl.v().ap, 0.0), writes=[tl.v()])
        A("pool", lambda e, l=l: e.memset(SSB[l].v().ap, 0.0), writes=[SSB[l].v()])

    A("pool", lambda e: e.memset(S16.v(30 * T, 34 * T).ap, 0.0), writes=[S16.v(30 * T, 34 * T)], cost=2)
    wlist = []
    for g_ in range(NG):
        for l_ in range(NL):
            chs = []
            if en[0]:
                chs += list(range(0, 10))
            if en[2]:
                chs += list(range(26, 43))
            if en[1]:
                chs += list(range(10, 26))
            chs += list(range(43, 59))
            wlist += [(l_, c_) for c_ in chs]
    wstate = {"issued": 0, "j": 0}
    cvj = 0
    for l_ in range(NL):
        for c_ in range(NCH):
            i_ = l_ * NCH + c_
            A("pool", lambda e, l_=l_, c_=c_: e.dma_start(out=wbf[l_, c_], in_=wall[l_, c_]),
              writes=[WBF.v(i_, i_ + 1), CVR.v(cvj % 8, cvj % 8 + 1)], dma=f"cv{cvj % 8}", cost=9.0, name=f"cv{i_}")
            cvj += 1

    def next_w():
        j = wstate["j"]
        wstate["j"] += 1
        while wstate["issued"] < min(j + NS, len(wlist)):
            i = wstate["issued"]
            lyr, ch = wlist[i]
            s = i % NS
            A("sp", lambda e, s=s, lyr=lyr, ch=ch: e.dma_start(out=WR[s].v().ap, in_=wbf[lyr, ch]),
              reads=[WBF.v(lyr * NCH + ch, lyr * NCH + ch + 1)], writes=[WR[s].v()], dma=f"w{s}", cost=2.5, name=f"w{i}")
            wstate["issued"] += 1
        return WR[j % NS]

    pend = []

    def later(fn):
        pend.append(fn)

    def flush():
        while pend:
            pend.pop(0)()

    def proj(bank, M=128, coff=0, w=None, do_flush=True):
        if w is None:
            w = next_w()
        out = bank.v(0, T, 0, M)
        for k in range(16):
            lh = w.v(k * 128 + coff, k * 128 + coff + M)
            rh = XB.v(k * T, (k + 1) * T)
            A("pe", lambda e, out=out, lh=lh, rh=rh, k=k: e.matmul(out.ap, lhsT=lh.ap, rhs=rh.ap, start=(k == 0), stop=(k == 15)),
              reads=[lh, rh], writes=[out], cost=mmc(T))
        if do_flush:
            flush()
        return w

    def proj_pieces(post, npiece=4):
        stt_ = {}
        per = 16 // npiece

        def piece(i):
            if i == 0:
                stt_["bk"] = pbank()
                stt_["w"] = next_w()
            bk, w = stt_["bk"], stt_["w"]
            out = bk.v(0, T)
            for k in range(i * per, (i + 1) * per):
                lh = w.v(k * 128, k * 128 + 128)
                rh = XB.v(k * T, (k + 1) * T)
                A("pe", lambda e, out=out, lh=lh, rh=rh, k=k: e.matmul(out.ap, lhsT=lh.ap, rhs=rh.ap, start=(k == 0), stop=(k == 15)),
                  reads=[lh, rh], writes=[out], cost=mmc(T))
            if i == npiece - 1:
                flush()
                post(bk)
        return [lambda i=i: piece(i) for i in range(npiece)]

    def act(out, in_, func, cost=None, reads=None, **kw):
        if func == AF.Copy and kw:
            func = AF.Identity
        rd = [in_] + list(reads or [])
        A("act", lambda e: e.activation(out=out.ap, in_=in_.ap, func=func, **kw), reads=rd, writes=[out],
          cost=cost or actc(out.f1 - out.f0))

    def lnexp(out, in_, k, reads=None, **kw):
        act(out, in_, AF.Ln, reads=reads, **kw)
        act(out, out, AF.Exp, scale=k)

    def tt(eng, out, a, b, op, cost=None):
        c = cost or (dvec(out.f1 - out.f0) if eng == "dve" else poolc(out.f1 - out.f0))
        A(eng, lambda e: e.tensor_tensor(out=out.ap, in0=a.ap, in1=b.ap, op=op), reads=[a, b], writes=[out], cost=c)

    def ts(eng, out, a, s1, s2, op0, op1=None, reads=(), cost=None):
        c = cost or (dvec(out.f1 - out.f0) if eng == "dve" else poolc(out.f1 - out.f0))
        s1a = s1.ap if isinstance(s1, View) else s1
        s2a = s2.ap if isinstance(s2, View) else s2
        rd = [a] + [s_ for s_ in (s1, s2) if isinstance(s_, View)] + list(reads)
        if s2 is None:
            A(eng, lambda e: e.tensor_scalar(out=out.ap, in0=a.ap, scalar1=s1a, scalar2=None, op0=op0), reads=rd, writes=[out], cost=c)
        else:
            A(eng, lambda e: e.tensor_scalar(out=out.ap, in0=a.ap, scalar1=s1a, scalar2=s2a, op0=op0, op1=op1), reads=rd, writes=[out], cost=c)

    def stt(eng, out, a, s, b, op0, op1, cost=None):
        eng = "dve"
        c = cost or (dvec(out.f1 - out.f0) if eng == "dve" else poolc(out.f1 - out.f0))
        sa = s.ap if isinstance(s, View) else s
        rd = [a, b] + ([s] if isinstance(s, View) else [])
        A(eng, lambda e: e.scalar_tensor_tensor(out=out.ap, in0=a.ap, scalar=sa, in1=b.ap, op0=op0, op1=op1), reads=rd, writes=[out], cost=c)

    def mm(out, lh, rh, start=True, stop=True, fp32=False):
        A("pe", lambda e: e.matmul(out.ap, lhsT=lh.ap, rhs=rh.ap, start=start, stop=stop), reads=[lh, rh], writes=[out],
          cost=mmc(rh.f1 - rh.f0, fp32))

    def tr(out, in_, ident):
        A("pe", lambda e: e.transpose(out.ap, in_.ap, ident.ap), reads=[in_, ident], writes=[out], cost=0.12)

    def cp(eng, out, in_, cost=None):
        c = cost or (dvec(out.f1 - out.f0) if eng == "dve" else poolc(out.f1 - out.f0) if eng == "pool" else actc(out.f1 - out.f0))
        if eng == "act":
            A("act", lambda e: e.activation(out=out.ap, in_=in_.ap, func=AF.Copy), reads=[in_], writes=[out], cost=c)
        else:
            A(eng, lambda e: e.tensor_copy(out=out.ap, in_=in_.ap), reads=[in_], writes=[out], cost=c)

    def gstop(k, dumps=()):
        if GSTOP != k:
            return
        off = 0
        for v_ in dumps:
            w_ = v_.f1 - v_.f0
            dst_ = View(XF, v_.p0, v_.p1, off, off + w_)
            cp("dve", dst_, v_)
            off += w_
        A("sp", lambda e: e.dma_start(out=outT[:, :, 0:T].rearrange("m p t -> p m t"),
                                      in_=XF.v().ap.rearrange("p (m t) -> p m t", m=16)),
          reads=[XF.v()], dma="o0", cost=15.0)
        raise StopBuild()

    try:
        for g in range(NG):
            t0 = g * T
            for m_ in range(16):
                A("sp", lambda e, t0=t0, m_=m_: e.dma_start(out=XF.v(m_ * T, (m_ + 1) * T).ap, in_=xT[m_, :, t0:t0 + T]),
                  writes=[XF.v(m_ * T, (m_ + 1) * T)], dma=f"x{m_}", cost=1.5)
            A("sp", lambda e, t0=t0: e.dma_start(out=COS.v().ap, in_=cosT[:, t0:t0 + T]), writes=[COS.v()], dma="t0", cost=2.0)
            A("sp", lambda e, t0=t0: e.dma_start(out=SIN.v().ap, in_=sinT[:, t0:t0 + T]), writes=[SIN.v()], dma="t1", cost=2.0)
            for m_ in range(16):
                cp(("pool", "act", "dve", "act")[m_ % 4], XB.v(m_ * T, (m_ + 1) * T), XF.v(m_ * T, (m_ + 1) * T))
            for l in range(NL):
                cb = l * NCOL_L
                col = lambda c0, n=1: COLS.v(cb + c0, cb + c0 + n)
                d0 = l * 40
                qfs = [S32.v(i * 3 * T, i * 3 * T + T) for i in range(2)]
                t1s = [S32.v(i * 3 * T + T, i * 3 * T + 2 * T) for i in range(2)]
                t2s = [S32.v(i * 3 * T + 2 * T, i * 3 * T + 3 * T) for i in range(2)]
                qT = [S16.v((20 + c) * T, (21 + c) * T) for c in range(4)]
                gA = [S16.v((24 + c) * T, (25 + c) * T) for c in range(4)]
                vf = S16.v(28 * T, 29 * T)
                PT = [S16.v((29 + i) * T, (30 + i) * T) for i in range(2)]
                ysm = S32.v(10324, 10324 + 128); rdn = S32.v(10324 + 128, 10324 + 256)
                AT_steps = []

                def rope(bk, dsts, i):
                    qf, t1, t2 = qfs[i % 2], t1s[i % 2], t2s[i % 2]
                    act(qf, bk.v(0, T), AF.Copy)

                    def rest():
                        pb = mbank()
                        mm(pb.v(0, T), PERM, qf, fp32=True)
                        tt("pool", t1, qf, COS.v(), ALU.mult)
                        tt("dve", t2, pb.v(0, T), SIN.v(), ALU.mult)
                        for dst in dsts:
                            a1_ = View(t1.tile, dst.p0, dst.p1, t1.f0, t1.f1)
                            a2_ = View(t2.tile, dst.p0, dst.p1, t2.f0, t2.f1)
                            tt("dve", dst, a1_, a2_, ALU.add)
                    later(rest)

                if en[0]:
                    for c in range(4):
                        bk = pbank(); proj(bk)
                        rope(bk, [qT[c]], c)
                    bk = pbank(); proj(bk)
                    rope(bk, [KB[l][0].v(128, 128 + T, 0, 64), KB[l][1].v(128, 128 + T, 64, 128)], 4)
                    bk = pbank(); proj(bk)
                    act(vf, bk.v(0, T), AF.Copy)

                    def vtr():
                        for b in range(4):
                            pb = mbank()
                            mm(pb.v(0, 128), vf.sub(b * 128, b * 128 + 128), IDB.v())
                            s_ = (b + 1) * 256
                            cp("dve", VAB[l].v(s_, s_ + 64), pb.v(0, 64), cost=0.2)
                            cp("dve", VAB[l].v(s_ + 192, s_ + 256), pb.v(64, 128), cost=0.2)
                    later(vtr)
                    for c in range(4):
                        bk = pbank(); proj(bk)
                        tg = t1s[c % 2] if False else S32.v(6 * T + 256 + (c % 2) * T, 6 * T + 256 + (c % 2) * T + T)
                        act(tg, bk.v(0, T), AF.Tanh, scale=0.5)
                        stt("dve", gA[c], tg, 1.0, bk.v(0, T), ALU.add, ALU.mult)
                    flush()

                    def s_part(b, c, i):
                        first = (g == 0 and b == 0)
                        pa = mbank()
                        for hh in range(2):
                            for kb_ in range(2):
                                out = pa.v((hh * 2 + kb_) * 128, (hh * 2 + kb_) * 128 + 128)
                                lh = KB[l][hh].v((b + kb_) * 128, (b + kb_) * 128 + 128)
                                rh = qT[c].sub(b * 128, b * 128 + 128)
                                mm(out, lh, rh)
                        pt = PT[i % 2]
                        act(pt, pa.v(0, T), AF.Exp, scale=0.125)
                        mk = MSK.v(512, 1024) if first else MSK.v(0, 512)
                        tt("dve", pt, pt, mk, ALU.mult, cost=0.4)
                        return pt

                    def nd_part(b, c, pt):
                        pn = mbank()
                        for i, (hh, kb_) in enumerate(((0, 0), (0, 1), (1, 0), (1, 1))):
                            s_ = (b + kb_) * 256 + hh * 128
                            mm(pn.v(0, 128), VAB[l].v(s_, s_ + 128), pt.sub((hh * 2 + kb_) * 128, (hh * 2 + kb_) * 128 + 128), start=(i == 0), stop=(i == 3))
                        for i, (hh, kb_) in enumerate(((0, 0), (0, 1), (1, 0), (1, 1))):
                            mm(pn.v(128, 256), ONAB.v(hh * 128, hh * 128 + 128), pt.sub((hh * 2 + kb_) * 128, (hh * 2 + kb_) * 128 + 128), start=(i == 0), stop=(i == 3))
                        es_ = DER.v(d0 + 16 + c, d0 + 17 + c)
                        lnexp(rdn, pn.v(128, 256), -1.0, reads=[es_], bias=es_.ap, cost=0.33)
                        tt("dve", ysm, pn.v(0, 128), rdn, ALU.mult, cost=0.3)
                        yo = YT.v(c * T + b * 128, c * T + b * 128 + 128)
                        stt("dve", yo, ysm, 0.5, gA[c].sub(b * 128, b * 128 + 128), ALU.mult, ALU.mult, cost=0.3)

                    its_ = [(b, c) for b in range(4) for c in range(4)]
                    ptl = {}

                    def at_s(i):
                        ptl[i] = s_part(its_[i][0], its_[i][1], i)

                    def at_nd(i):
                        nd_part(its_[i][0], its_[i][1], ptl[i])
                    for i in range(16):
                        AT_steps.append(lambda i=i: at_s(i))
                        if i > 0:
                            AT_steps.append(lambda i=i: at_nd(i - 1))
                    AT_steps.append(lambda: at_nd(15))

                    def at_roll():
                        cp("pool", KB[l][0].v(0, 128, 0, 64), KB[l][0].v(T, T + 128, 0, 64), cost=0.3)
                        cp("pool", KB[l][1].v(0, 128, 64, 128), KB[l][1].v(T, T + 128, 64, 128), cost=0.3)
                        cp("pool", VAB[l].v(0, 256), VAB[l].v(1024, 1280), cost=0.4)
                    AT_steps.append(at_roll)
                else:
                    A("pool", lambda e: e.memset(YT.v(0, 4 * T).ap, 0.0), writes=[YT.v(0, 4 * T)], cost=2)
                R_steps = []
                if en[1]:
                    RXB = S32.v(0, 515); xr = S32.v(520, 520 + T); r_ = S32.v(1040, 1040 + T); tg = S32.v(1560, 1560 + T)
                    a_ = S32.v(2600, 2600 + T); a2 = S32.v(3120, 3120 + T); ig = S32.v(3640, 3640 + T); hh_ = S32.v(7760, 7760 + T)
                    xrb = S16.v(36 * T, 37 * T); gz = S16.v(37 * T, 38 * T)

                    def r_post_a(n, bk):
                        cp("pool", RXB.sub(0, 3), RXH[l].v(3 * n, 3 * n + 3), cost=0.15)
                        act(RXB.sub(3, 515), bk.v(0, T), AF.Copy)
                        cp("pool", RXH[l].v(3 * n, 3 * n + 3), RXB.sub(512, 515), cost=0.15)
                        ts("pool", xr, RXB.sub(3, 515), col(C_RCW + 4 * n + 3), col(C_RCB + n), ALU.mult, ALU.add)
                        for k in range(3):
                            stt("dve", xr, RXB.sub(k, k + T), col(C_RCW + 4 * n + k), xr, ALU.mult, ALU.add)
                        cp("act", xrb, xr)

                        def gates():
                            wa = WAB.v((l * 2) * 1024 + n * 128, (l * 2) * 1024 + n * 128 + 128)
                            wx = WAB.v((l * 2 + 1) * 1024 + n * 128, (l * 2 + 1) * 1024 + n * 128 + 128)
                            pga = mbank(); mm(pga.v(0, T), wa, xrb)
                            pgx = mbank(); mm(pgx.v(0, T), wx, xrb)
                            hba = DER.v(d0 + 20 + n, d0 + 21 + n); hbx = DER.v(d0 + 28 + n, d0 + 29 + n)
                            cnh = DER.v(d0 + n, d0 + n + 1); cn_ = DER.v(d0 + 8 + n, d0 + 9 + n)
                            act(r_, pga.v(0, T), AF.Tanh, scale=0.5, bias=hba.ap, reads=[hba])
                            act(ig, pgx.v(0, T), AF.Tanh, scale=0.5, bias=hbx.ap, reads=[hbx])
                            act(a_, r_, AF.Exp, scale=cnh.ap, bias=cnh.ap, reads=[cnh])
                            act(a2, r_, AF.Exp, scale=cn_.ap, bias=cn_.ap, reads=[cn_])
                            act(a2, a2, AF.Ln, scale=-1.0, bias=1.0)
                            act(a2, a2, AF.Exp, scale=0.5, bias=math.log(0.5))
                            stt("dve", ig, ig, 1.0, xr, ALU.add, ALU.mult)
                            tt("pool", ig, ig, a2, ALU.mult)
                            hs_ = HST[l].v(n, n + 1)
                            A("dve", lambda e: e.tensor_tensor_scan(out=hh_.ap, data0=a_.ap, data1=ig.ap, initial=hs_.ap, op0=ALU.mult, op1=ALU.add),
                              reads=[a_, ig, hs_], writes=[hh_], cost=dvec(T))
                            cp("pool", hs_, hh_.sub(T - 1, T), cost=0.15)
                        later(gates)

                    def r_post_b(n, bk):
                        act(tg, bk.v(0, T), AF.Tanh, scale=0.5)
                        stt("dve", gz, tg, 1.0, bk.v(0, T), ALU.add, ALU.mult)

                        def fin():
                            stt("dve", YT.v((4 + n) * T, (5 + n) * T), hh_, 0.5, gz, ALU.mult, ALU.mult)
                        later(fin)

                    for n in range(8):
                        R_steps += proj_pieces(lambda bk, n=n: r_post_a(n, bk))
                        R_steps += proj_pieces(lambda bk, n=n: r_post_b(n, bk))
                else:
                    A("pool", lambda e: e.memset(YT.v(4 * T, 12 * T).ap, 0.0), writes=[YT.v(4 * T, 12 * T)], cost=4)
                if en[2]:
                    GXB = S32.v(0, 515); cqkv = [S32.v(520 + i * 520, 520 + i * 520 + T) for i in range(3)]
                    sq = S32.v(2080, 2080 + T); rn = S32.v(2600, 2600 + T)
                    EGt = [S32.v(3120 + i * 520, 3120 + i * 520 + T) for i in range(2)]
                    DT = [S32.v(4160 + i * 520, 4160 + i * 520 + T) for i in range(2)]
                    BV = [S32.v(5200 + i * 512, 5200 + i * 512 + T) for i in range(4)]
                    DIAGB = S32.v(7248, 7248 + T)
                    tgA = S32.v(9300, 9300 + T); tgB = S32.v(9812, 9812 + T)
                    BETA = S32.v(7760, 7760 + T, 0, 4); GG = S32.v(8272, 8272 + T, 0, 4); GCUM = S32.v(8784, 8784 + T, 0, 4)
                    qn = [S16.v(i * T, (i + 1) * T) for i in range(4)]
                    kn = [S16.v((4 + i) * T, (5 + i) * T) for i in range(4)]
                    QD = [S16.v((8 + i) * T, (9 + i) * T) for i in range(4)]
                    gG = [S16.v((12 + i) * T, (13 + i) * T) for i in range(4)]
                    OT = [S16.v((16 + i) * T, (17 + i) * T) for i in range(4)]
                    KTOK = [S16.v((20 + i) * T, (21 + i) * T) for i in range(2)]
                    Qr = [S16.v((22 + i) * T, (23 + i) * T) for i in range(2)]
                    QTr = [S16.v((24 + i) * T, (25 + i) * T) for i in range(2)]
                    Xr = [S16.v((26 + i) * T, (27 + i) * T) for i in range(2)]
                    KQm = [S16.v((28 + i) * T, (29 + i) * T) for i in range(2)]
                    Rb = S16.v(30 * T, 31 * T); VN = S16.v(31 * T, 32 * T); VNS2 = [S16.v((32 + i) * T, (33 + i) * T) for i in range(2)]
                    XFIN = [S16.v((34 + i) * T, (35 + i) * T) for i in range(2)]
                    hs = lambda v, h, r0=0, r1=128: View(v.tile, r0, r1, v.f0 + h * 128, v.f0 + h * 128 + 128)
                    GCBC = SMALL.v(0, 32); NEGGC = SMALL.v(32, 64); EGC = SMALL.v(64, 96); NBEG = SMALL.v(96, 112)
                    EDL = SMALL.v(112, 144); GLC = SMALL.v(144, 176)
                    GH_steps = []

                    def g_bg():
                        w = next_w()
                        bkb = mbank(); proj(bkb, M=4, coff=0, w=w, do_flush=False)
                        bka = mbank(); proj(bka, M=4, coff=32, w=w)
                        act(BETA, bkb.v(0, T, 0, 4), AF.Tanh, scale=0.5)
                        ts("dve", BETA, BETA, 0.5, 0.5, ALU.mult, ALU.add)
                        dtb = SM4.v(l * 2 + 1, l * 2 + 2)
                        act(GG, bka.v(0, T, 0, 4), AF.Exp, bias=dtb.ap, reads=[dtb])
                        act(GG, GG, AF.Ln, bias=1.0)
                        ts("dve", GG, GG, NEGA.v(l * 2, l * 2 + 1), None, ALU.mult)
                        A("dve", lambda e: e.tensor_tensor_scan(out=GCUM.ap, data0=RST.ap, data1=GG.ap, initial=0.0, op0=ALU.mult, op1=ALU.add),
                          reads=[RST, GG], writes=[GCUM], cost=dvec(T))

                        def colforms():
                            pc_ = mbank()
                            for p in range(4):
                                mm(pc_.v(p * 8, p * 8 + 4), GCUM.sub(p * 128, p * 128 + 128), CST.v(0, 4, 0, 4), fp32=True)
                                mm(pc_.v(p * 8 + 4, p * 8 + 8), BETA.sub(p * 128, p * 128 + 128), CST.v(0, 4, 0, 4), fp32=True)
                            cp("dve", GCBC, pc_.v(0, 32), cost=0.2)
                            ts("dve", NEGGC, GCBC, -1.0, None, ALU.mult, cost=0.2)
                            act(EGC, GCBC, AF.Exp, cost=0.3)
                            for p in range(4):
                                stt("dve", NBEG.sub(p * 4, p * 4 + 4), GCBC.sub(p * 8 + 4, p * 8 + 8), -1.0, EGC.sub(p * 8, p * 8 + 4), ALU.mult, ALU.mult, cost=0.2)
                            pl_ = mbank()
                            mm(pl_.v(0, 32), LASTSEL, GCBC, fp32=True)
                            tt("dve", EDL, pl_.v(0, 32), GCBC, ALU.subtract, cost=0.2)
                            act(EDL, EDL, AF.Exp, cost=0.3)
                        later(colforms)
                    GH_steps.append(g_bg)

                    def g_conv_post(h, qi, bk):
                        dst = cqkv[qi]
                        ci = qi * 4 + h
                        cp("pool", GXB.sub(0, 3), GXH[l].v(3 * ci, 3 * ci + 3), cost=0.15)
                        act(GXB.sub(3, 515), bk.v(0, T), AF.Copy)
                        cp("pool", GXH[l].v(3 * ci, 3 * ci + 3), GXB.sub(512, 515), cost=0.15)
                        ts("pool", dst, GXB.sub(3, 515), col(C_GCW + 4 * ci + 3), 0.0, ALU.mult, ALU.add)
                        for k in range(3):
                            stt("dve", dst, GXB.sub(k, k + T), col(C_GCW + 4 * ci + k), dst, ALU.mult, ALU.add)
                        act(tgA, dst, AF.Tanh, scale=0.5)
                        stt("dve", dst, tgA, 1.0, dst, ALU.add, ALU.mult)
                        if qi == 2:
                            later(lambda: headpost(h))

                    def headpost(h):
                        for src, dstb, scl in ((cqkv[0], qn[h], 0.5 * 128 ** -0.5), (cqkv[1], kn[h], 0.5)):
                            act(sq, src, AF.Square, scale=0.5)
                            pb = mbank(); mm(pb.v(0, T), ONESF, sq, fp32=True)
                            lnexp(rn, pb.v(0, T), -0.5, bias=RMS_EPS)
                            stt("dve", dstb, src, scl, rn, ALU.mult, ALU.mult)
                        pv = mbank()
                        for p in range(4):
                            mm(pv.v(p * 128, p * 128 + 128), cqkv[2].sub(p * 128, p * 128 + 128), IDENT, fp32=True)
                        for p in range(4):
                            bc = GCBC.sub(p * 8 + 4 + h, p * 8 + 5 + h)
                            ts("dve", hs(BV[p], h), pv.v(p * 128, p * 128 + 128), bc, 0.5, ALU.mult, ALU.mult, cost=0.3)
                        pg = mbank()
                        mm(pg.v(0, T), SEL.sub(h * 128, h * 128 + 128), GCUM, fp32=True)
                        eg = EGt[h % 2]
                        act(eg, pg.v(0, T), AF.Exp)
                        tt("dve", QD[h], qn[h], eg, ALU.mult)
                        glv = GLC.sub(h * 8, h * 8 + 8)
                        A("pool", lambda e, eg=eg, glv=glv: e.tensor_copy(out=glv.ap, in_=eg.ap.rearrange("p (c t) -> p c t", t=64)[:, :, 63]),
                          reads=[eg], writes=[glv], cost=0.2)

                    def g_gate_post(h, bk):
                        act(tgB, bk.v(0, T), AF.Tanh, scale=0.5)
                        stt("dve", gG[h], tgB, 1.0, bk.v(0, T), ALU.add, ALU.mult)

                    for h in range(4):
                        for qi in range(3):
                            GH_steps += proj_pieces(lambda bk, h=h, qi=qi: g_conv_post(h, qi, bk))
                        GH_steps += proj_pieces(lambda bk, h=h: g_gate_post(h, bk))
                    ai = 0
                    for gi, gs in enumerate(GH_steps):
                        gs()
                        if gi % 2 == 0 and ai < len(AT_steps):
                            AT_steps[ai](); ai += 1
                    while ai < len(AT_steps):
                        AT_steps[ai](); ai += 1
                    flush()
                    GP_steps = []

                    def pair_steps(p):
                        dt_ = DT[p % 2]

                        def s1():
                            pm = mbank()
                            for h in range(4):
                                mm(hs(pm.v(0, T), h), SEL.sub(h * 128, h * 128 + 128), GCUM.sub(p * 128, p * 128 + 128), start=True, stop=False, fp32=True)
                                mm(hs(pm.v(0, T), h), IDB.v(), NMK.v(0, 128), start=False, stop=True)
                                ng = NEGGC.sub(p * 8 + h, p * 8 + h + 1)
                                act(hs(dt_, h), hs(pm.v(0, T), h), AF.Exp, bias=ng.ap, reads=[ng], cost=0.35)
                        GP_steps.append(s1)

                        def s2():
                            pKK = mbank(); pKQ = mbank(); pBR = mbank()
                            for h in range(4):
                                kp = kn[h].sub(p * 128, p * 128 + 128)
                                mm(hs(pKK.v(0, T), h), kp, kp)
                                mm(hs(pKQ.v(0, T), h), kp, qn[h].sub(p * 128, p * 128 + 128))
                            bcs = GCBC.sub(p * 8 + 4, p * 8 + 8)
                            A("dve", lambda e, bcs=bcs: e.tensor_tensor(out=DIAGB.r3(4), in0=IDENT.ap.unsqueeze(1).to_broadcast([128, 4, 128]),
                                                                       in1=bcs.ap.unsqueeze(2).to_broadcast([128, 4, 128]), op=ALU.mult),
                              reads=[IDENT, bcs], writes=[DIAGB], cost=dvec(T))
                            for h in range(4):
                                mm(hs(pBR.v(0, T), h), OFFD, hs(DIAGB, h), fp32=True)
                            tt("dve", sq, pKK.v(0, T), dt_, ALU.mult)
                            tt("dve", Qr[0], sq, pBR.v(0, T), ALU.mult)
                            tt("dve", KQm[p % 2], pKQ.v(0, T), dt_, ALU.mult)
                            A("pool", lambda e: e.tensor_tensor(out=Xr[0].r3(4), in0=Qr[0].r3(4), in1=IDB.v().ap.unsqueeze(1).to_broadcast([128, 4, 128]), op=ALU.add),
                              reads=[Qr[0], IDB.v()], writes=[Xr[0]], cost=poolc(T))
                        GP_steps.append(s2)

                        def s3():
                            pt_ = mbank()
                            for h in range(4):
                                mm(hs(pt_.v(0, T), h), hs(Qr[0], h), IDB.v())
                            cp("act", QTr[0], pt_.v(0, T))
                            pk_ = mbank()
                            for h in range(4):
                                mm(hs(pk_.v(0, T), h), kn[h].sub(p * 128, p * 128 + 128), IDB.v())
                            cp("act", KTOK[p % 2], pk_.v(0, T))
                        GP_steps.append(s3)

                        def lvl(k):
                            Qp, QpT = Qr[(k - 1) % 2], QTr[(k - 1) % 2]
                            Qn_, QnT = Qr[k % 2], QTr[k % 2]
                            Xp = Xr[(k - 1) % 2]
                            Xn_ = XFIN[p % 2] if k == 5 else Xr[k % 2]
                            pa_ = mbank()
                            for h in range(4):
                                mm(hs(pa_.v(0, T), h), hs(Qp, h), hs(QpT, h))
                            cp("act", QnT, pa_.v(0, T))
                            if k < 5:
                                pb_ = mbank()
                                for h in range(4):
                                    mm(hs(pb_.v(0, T), h), hs(QpT, h), hs(Qp, h))
                                cp("dve", Qn_, pb_.v(0, T))
                            px_ = mbank()
                            for h in range(4):
                                mm(hs(px_.v(0, T), h), hs(QnT, h), hs(Xp, h))
                            tt("dve", Xn_, px_.v(0, T), Xp, ALU.add)
                        for k in range(1, 6):
                            GP_steps.append(lambda k=k: lvl(k))

                        def chunk_a(cc):
                            r0 = (cc % 2) * 64; r1 = r0 + 64; lc = r0; gc0 = cc * 64
                            pKS = mbank()
                            for h in range(4):
                                mm(hs(pKS.v(0, T), h, r0, r1), kn[h].sub(gc0, gc0 + 64), hs(SSB[l].v(), h))
                            for h in range(4):
                                nb = View(SMALL, r0, r1, 96 + p * 4 + h, 96 + p * 4 + h + 1)
                                stt("dve", hs(Rb, h, r0, r1), hs(pKS.v(0, T), h, r0, r1), nb, hs(BV[p], h, r0, r1), ALU.mult, ALU.add, cost=0.3)
                            pVN = mbank()
                            for h in range(4):
                                lh = XFIN[p % 2].sub(h * 128 + lc, h * 128 + lc + 64)
                                mm(hs(pVN.v(0, T), h, r0, r1), lh, hs(Rb, h))
                            cp("act", View(VN.tile, r0, r1, VN.f0, VN.f1), pVN.v(0, T, r0, r1))
                            VNS = VNS2[cc % 2]
                            for h in range(4):
                                ed = View(SMALL, r0, r1, 112 + p * 8 + h, 112 + p * 8 + h + 1)
                                ts("dve", hs(VNS, h, r0, r1), hs(pVN.v(0, T), h, r0, r1), ed, None, ALU.mult, cost=0.3)

                        def chunk_b(cc):
                            r0 = (cc % 2) * 64; r1 = r0 + 64; lc = r0; gc0 = cc * 64
                            VNS = VNS2[cc % 2]
                            pO = mbank()
                            for h in range(4):
                                o_ = pO.v(h * 64, h * 64 + 64)
                                mm(o_, hs(SSB[l].v(), h), QD[h].sub(gc0, gc0 + 64), start=True, stop=False)
                                rh = KQm[p % 2].sub(h * 128 + lc, h * 128 + lc + 64)
                                mm(o_, hs(VN, h), rh, start=False, stop=True)
                            for h in range(4):
                                cp("act" if h % 2 else "dve", OT[h].sub(gc0, gc0 + 64), pO.v(h * 64, h * 64 + 64), cost=0.25)
                            pDS = mbank()
                            for h in range(4):
                                mm(hs(pDS.v(0, T), h), hs(KTOK[p % 2], h), hs(VNS, h))
                            for h in range(4):
                                gl = GLC.sub(h * 8 + cc, h * 8 + cc + 1)
                                stt("dve", hs(SST[l].v(), h), hs(SST[l].v(), h), gl, hs(pDS.v(0, T), h), ALU.mult, ALU.add, cost=0.3)
                            cp("act", SSB[l].v(), SST[l].v())
                        for cc in (2 * p, 2 * p + 1):
                            GP_steps.append(lambda cc=cc: chunk_a(cc))
                            GP_steps.append(lambda cc=cc: chunk_b(cc))

                    for p in range(4):
                        pair_steps(p)
                    ri = 0
                    for gi, gs in enumerate(GP_steps):
                        for _ in range(2 if gi % 3 == 0 else 1):
                            if ri < len(R_steps):
                                R_steps[ri](); ri += 1
                        gs()
                    while ri < len(R_steps):
                        R_steps[ri](); ri += 1
                    flush()
                    gstop(4, [OT[0], OT[1], SST[l].v()])
                    for h in range(4):
                        act(sq, OT[h], AF.Square)
                        pb = mbank(); mm(pb.v(0, T), ONESF, sq, fp32=True)
                        lnexp(rn, pb.v(0, T), -0.5, scale=1.0 / 128, bias=RMS_EPS)
                        stt("dve", cqkv[0], OT[h], 0.5, rn, ALU.mult, ALU.mult)
                        stt("dve", YT.v((12 + h) * T, (13 + h) * T), cqkv[0], col(C_GNW), gG[h], ALU.mult, ALU.mult)
                else:
                    A("pool", lambda e: e.memset(YT.v(12 * T, 16 * T).ap, 0.0), writes=[YT.v(12 * T, 16 * T)], cost=2)
                    for as_ in AT_steps:
                        as_()
                    for rs_ in R_steps:
                        rs_()
                    flush()
                if DBG:
                    cp("dve", XF.v(0, 4 * T), YT.v(0, 4 * T))
                    cp("dve", XF.v(4 * T, 5 * T), S16.v(0, T))
                    cp("dve", XF.v(5 * T, 6 * T), KB[l][0].v(128, 128 + T))
                    cp("dve", XF.v(6 * T, 7 * T), S16.v(4 * T, 5 * T))
                    cp("dve", XF.v(7 * T, 8 * T), S16.v(9 * T, 10 * T))
                    A("sp", lambda e, t0=t0: e.dma_start(out=outT[:, :, t0:t0 + T].rearrange("m p t -> p m t"),
                                                         in_=XF.v().ap.rearrange("p (m t) -> p m t", m=16)),
                      reads=[XF.v()], dma="o0", cost=15.0)
                    return nc, P, st
                ps1 = mbank(); ps2 = mbank()

                def stats(m):
                    zbm = S16.v((2 * (m % 2)) * T, (2 * (m % 2) + 1) * T); zsm = S16.v((2 * (m % 2) + 1) * T, (2 * (m % 2) + 2) * T)
                    mm(ps1.v(0, T), ONB.v(), zbm, start=(m == 0), stop=(m == 15))
                    mm(ps2.v(0, T), ONB.v(), zsm, start=(m == 0), stop=(m == 15))

                for m in range(16):
                    bk = pbank()
                    w = next_w()
                    for k in range(16):
                        lh = w.v(k * 128, k * 128 + 128); rh = YT.v(k * T, (k + 1) * T)
                        A("pe", lambda e, bk=bk, lh=lh, rh=rh, k=k: e.matmul(bk.v(0, T).ap, lhsT=lh.ap, rhs=rh.ap, start=(k == 0), stop=(k == 15)),
                          reads=[lh, rh], writes=[bk.v(0, T)], cost=mmc(T))
                    if m > 0:
                        stats(m - 1)
                    xm = XF.v(m * T, (m + 1) * T)
                    stt("dve", xm, xm, ALPHA, bk.v(0, T), ALU.mult, ALU.add)
                    zbm = S16.v((2 * (m % 2)) * T, (2 * (m % 2) + 1) * T); zsm = S16.v((2 * (m % 2) + 1) * T, (2 * (m % 2) + 2) * T)
                    cp("dve", zbm, xm)
                    act(zsm, xm, AF.Square)
                stats(15)
                mean = S32.v(0, T); msq = S32.v(T, 2 * T); rstd = S32.v(2 * T, 3 * T); mr = S32.v(3 * T, 4 * T)
                tmpa = S32.v(4 * T, 5 * T); tmpb = S32.v(5 * T, 6 * T)
                act(mean, ps1.v(0, T), AF.Copy, scale=1.0 / 2048)
                act(msq, mean, AF.Square)
                stt("dve", rstd, ps2.v(0, T), 1.0 / 2048, msq, ALU.mult, ALU.subtract)
                lnexp(rstd, rstd, -0.5, bias=LN_EPS)
                tt("dve", mr, mean, rstd, ALU.mult)
                last = (l == NL - 1)
                for m in range(16):
                    xm = XF.v(m * T, (m + 1) * T)
                    tm = tmpa if m % 2 == 0 else tmpb
                    tt("dve", tm, xm, rstd, ALU.mult)
                    tt("dve" if m % 4 else "pool", tm, tm, mr, ALU.subtract)
                    lg = col(C_LNG + m); lb = col(C_LNB + m)
                    act(xm, tm, AF.Identity, scale=lg.ap, bias=lb.ap, reads=[lg, lb])
                    if not last:
                        cp("pool" if m % 2 else "act", XB.v(m * T, (m + 1) * T), xm)
                    else:
                        A("sp", lambda e, t0=t0, m=m, xm=xm: e.dma_start(out=outT[m, :, t0:t0 + T], in_=xm.ap),
                          reads=[xm], dma=f"o{m}", cost=1.5)
    except StopBuild:
        pass
    return nc, P, st


def _chunk_cols():
    ch = []
    def heads(base, c):
        return list(range(base + c * 64, base + c * 64 + 64)) + list(range(base + (4 + c) * 64, base + (4 + c) * 64 + 64))
    for c in range(4):
        ch.append(heads(0, c))
    ch.append(list(range(512, 640)))
    ch.append(list(range(640, 768)))
    for c in range(4):
        ch.append(heads(768, c))
    for n in range(8):
        ch.append(list(range(1280 + n * 128, 1280 + n * 128 + 128)))
        ch.append(list(range(2304 + n * 128, 2304 + n * 128 + 128)))
    bg = [-1] * 128
    for i in range(4):
        bg[i] = 5376 + i
        bg[32 + i] = 5380 + i
    ch.append(bg)
    for h in range(4):
        for base in (3328, 3840, 4352, 4864):
            ch.append(list(range(base + h * 128, base + h * 128 + 128)))
    assert len(ch) == NCH_IN
    return ch


def _yt_rows():
    rows = []
    for kk in range(16):
        if kk < 4:
            rows += list(range(kk * 64, kk * 64 + 64)) + list(range((4 + kk) * 64, (4 + kk) * 64 + 64))
        elif kk < 12:
            rows += list(range(512 + (kk - 4) * 128, 512 + (kk - 4) * 128 + 128))
        else:
            rows += list(range(1536 + (kk - 12) * 128, 1536 + (kk - 12) * 128 + 128))
    return np.array(rows)


def prep_shared(inp, SEQ, NL=2):
    f = np.float32
    w_in = np.asarray(inp["w_in"], f); w_out = np.asarray(inp["w_out"], f)
    wall = np.zeros((NL, NCH, 128, 2048), f)
    chs = _chunk_cols()
    yr = _yt_rows()
    for l in range(NL):
        wz = np.concatenate([w_in[l], np.zeros((2048, 1), f)], axis=1)
        for j, cols in enumerate(chs):
            wc = wz[:, np.array(cols)]
            wall[l, j] = wc.reshape(16, 128, 128).transpose(1, 0, 2).reshape(128, 2048)
        wo = w_out[l][yr]
        for m in range(16):
            wall[l, NCH_IN + m] = wo[:, m * 128:(m + 1) * 128].reshape(16, 128, 128).transpose(1, 0, 2).reshape(128, 2048)
    inv = (1.0 / (10000.0 ** (np.arange(0, 64, 2, dtype=f) / f(64)))).astype(f)
    ang = (np.arange(SEQ, dtype=f)[:, None] * inv[None, :]).astype(f)
    cs, sn = np.cos(ang).astype(f), np.sin(ang).astype(f)
    cosT = np.zeros((128, SEQ), f); sinT = np.zeros((128, SEQ), f)
    for p in range(128):
        d = p % 64
        cosT[p] = cs[:, d % 32]
        sinT[p] = -sn[:, d % 32] if d < 32 else sn[:, d % 32]
    wab = np.stack([np.stack([np.asarray(inp["r_wa"], f)[l].transpose(1, 0, 2).reshape(128, 1024),
                              np.asarray(inp["r_wx"], f)[l].transpose(1, 0, 2).reshape(128, 1024)]) for l in range(NL)])
    cols = np.zeros((128, NL * NCOL_L), f)
    sm4 = np.zeros((4, NL * 2), f)
    for l in range(NL):
        cb = l * NCOL_L
        rcw = np.asarray(inp["r_conv_w"], f)[l]
        for n in range(8):
            for k in range(4):
                cols[:, cb + C_RCW + 4 * n + k] = rcw[k, n * 128:(n + 1) * 128]
            cols[:, cb + C_RCB + n] = np.asarray(inp["r_conv_b"], f)[l, n * 128:(n + 1) * 128]
            cols[:, cb + C_BA + n] = np.asarray(inp["r_ba"], f)[l, n * 128:(n + 1) * 128]
            cols[:, cb + C_BX + n] = np.asarray(inp["r_bx"], f)[l, n * 128:(n + 1) * 128]
            cols[:, cb + C_LAM + n] = np.asarray(inp["r_lam"], f)[l, n * 128:(n + 1) * 128]
        gcw = np.asarray(inp["g_conv_w"], f)[l]
        for ci in range(12):
            for k in range(4):
                cols[:, cb + C_GCW + 4 * ci + k] = gcw[k, ci * 128:(ci + 1) * 128]
        for m in range(16):
            cols[:, cb + C_LNG + m] = np.asarray(inp["ln_g"], f)[l, m * 128:(m + 1) * 128]
            cols[:, cb + C_LNB + m] = np.asarray(inp["ln_b"], f)[l, m * 128:(m + 1) * 128]
        cols[:, cb + C_GNW] = np.asarray(inp["g_norm_w"], f)[l]
        sk = np.asarray(inp["sinks"], f)[l]
        for c in range(4):
            cols[:64, cb + C_SINK + c] = sk[c]
            cols[64:, cb + C_SINK + c] = sk[4 + c]
        sm4[:, 2 * l] = np.asarray(inp["g_a_log"], f)[l]
        sm4[:, 2 * l + 1] = np.asarray(inp["g_dt_bias"], f)[l]
    cst = np.zeros((128, 9 * 128), f)
    I = np.eye(128, dtype=f)
    cst[:, 0:128] = I
    perm = np.zeros((128, 128), f)
    for m in range(128):
        d = m % 64
        perm[(m + 32) if d < 32 else (m - 32), m] = 1.0
    cst[:, 128:256] = perm
    cst[:, 256:384] = 1.0
    cst[:, 384:512] = I - 1.0
    ls = np.zeros((128, 128), f); ls[63, :64] = 1.0; ls[127, 64:] = 1.0
    cst[:, 512:640] = ls
    jj, ii = np.meshgrid(np.arange(128), np.arange(128), indexing="ij")
    cst[:, 640:768] = (jj > ii).astype(f)
    cst[:, 768:896] = (jj <= ii).astype(f)
    cst[:, 896:1024] = np.where((jj // 64 == ii // 64) & (ii >= jj), 0.0, -30000.0).astype(f)
    cst4 = np.zeros((4, 1024), f)
    for h in range(4):
        cst4[h, h * 128:(h + 1) * 128] = 1.0
    rst = np.ones(512, f); rst[::64] = 0.0
    cst4[:, 512:] = rst[None, :]
    return dict(wall=wall, cosT=cosT, sinT=sinT, wab=wab, cols=cols, sm4=sm4, cst=cst, cst4=cst4)


_CACHE = {}


def run_module(inp, SEQ, NL=2, en=(1, 1, 1), trace=False):
    x = np.asarray(inp["x"], np.float32)
    B = x.shape[0]
    shared = prep_shared(inp, SEQ, NL)
    in_maps = []
    for b in range(B):
        m = dict(shared)
        m["xT"] = np.ascontiguousarray(x[b].T).reshape(16, 128, SEQ)
        in_maps.append(m)
    key = (SEQ, NL, tuple(en))
    nc, P, st = build_program(SEQ, NL, en)
    P.schedule(WINDOW)
    P.emit()
    st.close()
    res = run_bass_kernel_spmd(nc, in_maps, core_ids=list(range(B)))
    out = np.stack([np.ascontiguousarray(res.results[b]["outT"].reshape(2048, SEQ).T) for b in range(B)])
    return out.astype(np.float32)


def kernel(**inputs):
    return run_module(inputs, 8192, 2, (1, 1, 1))
```

```python
import math
import numpy as np
import concourse.bass as bass
import concourse.mybir as mybir
from contextlib import ExitStack
from concourse.bass_utils import run_bass_kernel_spmd

F32 = mybir.dt.float32
BF16 = mybir.dt.bfloat16
ALU = mybir.AluOpType
AF = mybir.ActivationFunctionType

ENGS = ("pe", "act", "dve", "pool", "sp")
SAME_ENG_SKIP = {"act": 3, "dve": 3, "pool": 10 ** 9, "sp": 3}
WINDOW = 64


class View:
    __slots__ = ("tile", "p0", "p1", "f0", "f1", "_ap")

    def __init__(self, tile, p0, p1, f0, f1):
        self.tile, self.p0, self.p1, self.f0, self.f1 = tile, p0, p1, f0, f1
        self._ap = None

    @property
    def ap(self):
        if self._ap is None:
            self._ap = self.tile.h[self.p0:self.p1, self.f0:self.f1]
        return self._ap

    def sub(self, f0, f1, p0=None, p1=None):
        np0 = self.p0 if p0 is None else self.p0 + p0
        np1 = self.p1 if p1 is None else self.p0 + p1
        return View(self.tile, np0, np1, self.f0 + f0, self.f0 + f1)

    def r3(self, a):
        return self.ap.rearrange("p (a b) -> p a b", a=a)


class Tile:
    def __init__(self, name, h, P, F, space):
        self.name, self.h, self.P, self.F, self.space = name, h, P, F, space
        self.recs = []

    def v(self, f0=0, f1=None, p0=0, p1=None):
        return View(self, p0, self.P if p1 is None else p1, f0, self.F if f1 is None else f1)


class DramRegion:
    pass


class Op:
    __slots__ = ("id", "eng", "fn", "deps", "cost", "sig", "dma_sem", "dma_cnt", "start", "end", "name")


class Prog:
    def __init__(self, nc, stack):
        self.nc, self.stack = nc, stack
        self.ops = []
        self.tiles = {}
        self.dma_sems = {}

    def sbuf(self, name, P, F, dt):
        h = self.stack.enter_context(self.nc.sbuf_tensor(name, [P, F], dt))
        t = Tile(name, h, P, F, "sb")
        self.tiles[name] = t
        return t

    def psum(self, name, P, F, dt=F32):
        h = self.stack.enter_context(self.nc.psum_tensor(name, [P, F], dt))
        t = Tile(name, h, P, F, "ps")
        self.tiles[name] = t
        return t

    def dsem(self, name):
        if name not in self.dma_sems:
            h = self.stack.enter_context(self.nc.semaphore("d_" + name))
            self.dma_sems[name] = [h, 0]
        return name

    def _deps(self, op, reads, writes):
        deps = set()
        ps = [v for v in list(reads) + list(writes) if v.tile.space == "ps"]
        reads = [v for v in reads if v.tile.space != "ps"]
        writes = [v for v in writes if v.tile.space != "ps"]
        seen = set()
        for v in ps:
            if v.tile.name not in seen:
                seen.add(v.tile.name)
                writes.append(v.tile.v())
        for v in reads:
            recs = v.tile.recs
            for r in recs:
                if r[5] and r[0] < v.p1 and v.p0 < r[1] and r[2] < v.f1 and v.f0 < r[3]:
                    deps.add(r[4])
            recs.append([v.p0, v.p1, v.f0, v.f1, op.id, False])
        for v in writes:
            recs = v.tile.recs
            keep = []
            for r in recs:
                if r[0] < v.p1 and v.p0 < r[1] and r[2] < v.f1 and v.f0 < r[3]:
                    if r[4] != op.id:
                        deps.add(r[4])
                    if r[0] >= v.p0 and r[1] <= v.p1 and r[2] >= v.f0 and r[3] <= v.f1:
                        continue
                keep.append(r)
            keep.append([v.p0, v.p1, v.f0, v.f1, op.id, True])
            v.tile.recs = keep
        deps.discard(op.id)
        return deps

    def add(self, eng, fn, reads=(), writes=(), cost=0.3, dma=None, name=""):
        op = Op()
        op.id = len(self.ops)
        op.eng, op.fn, op.cost, op.name = eng, fn, cost, name
        op.sig = False
        op.dma_sem = dma
        op.dma_cnt = 0
        op.deps = self._deps(op, reads, writes)
        self.ops.append(op)
        return op

    def schedule(self, window=24):
        ops = self.ops
        per = {e: [o for o in ops if o.eng == e] for e in ENGS}
        pos = {e: 0 for e in ENGS}
        done = [False] * len(ops)
        endt = [0.0] * len(ops)
        free = {e: 0.0 for e in ENGS}
        order = {e: [] for e in ENGS}
        pend = {e: list(per[e]) for e in ENGS}
        n_left = len(ops)
        LAT = 0.15
        while n_left:
            best = None
            for e in ENGS:
                lst = pend[e]
                if not lst:
                    continue
                lim = 1 if e == "pe" else min(window, len(lst))
                for i in range(lim):
                    o = lst[i]
                    ok = True
                    rt = 0.0
                    for d in o.deps:
                        if not done[d]:
                            ok = False
                            break
                        t = endt[d] + (LAT if ops[d].eng != e or e != "pe" else 0.0)
                        if t > rt:
                            rt = t
                    if not ok:
                        continue
                    st = max(rt, free[e])
                    key = (st, o.id)
                    if best is None or key < best[0]:
                        best = (key, e, i, o)
                    if st <= free[e]:
                        break
            assert best is not None, "scheduler deadlock"
            (st, _), e, i, o = best
            pend[e].pop(i)
            o.start = st
            dur = o.cost
            if o.dma_sem is not None:
                free[e] = st + 0.1
                endt[o.id] = st + 2.0 + dur
            else:
                free[e] = st + dur
                endt[o.id] = st + dur
            o.end = endt[o.id]
            done[o.id] = True
            order[e].append(o)
            n_left -= 1
        self.order = order
        self.est = max(endt) if endt else 0.0
        return order

    def emit(self):
        nc = self.nc
        ops = self.ops
        order = self.order
        idx_in_eng = {}
        for e in ENGS:
            for k, o in enumerate(order[e]):
                idx_in_eng[o.id] = k
        for o in ops:
            for d in o.deps:
                do = ops[d]
                if do.dma_sem is not None:
                    continue
                if do.eng == o.eng:
                    if o.eng == "pe":
                        continue
                    if idx_in_eng[o.id] - idx_in_eng[d] > SAME_ENG_SKIP.get(o.eng, 3):
                        continue
                do.sig = True
        sems = {e: self.stack.enter_context(nc.semaphore("s_" + e)) for e in ENGS}
        cnt = {}
        for e in ENGS:
            c = 0
            for o in order[e]:
                if o.dma_sem is not None:
                    s = self.dma_sems[o.dma_sem]
                    s[1] += 16
                    o.dma_cnt = s[1]
                elif o.sig:
                    c += 1
                    cnt[o.id] = c
        sem_eng = {}
        for o in ops:
            if o.dma_sem is not None:
                assert sem_eng.setdefault(o.dma_sem, o.eng) == o.eng, "dma sem used from two engines"
        self.n_wait = 0

        def emit_engine(e, eh):
            waited = {}
            for o in order[e]:
                need = {}
                for d in o.deps:
                    do = ops[d]
                    if do.dma_sem is not None:
                        key = ("d", do.dma_sem)
                        val = do.dma_cnt
                    else:
                        if not do.sig:
                            continue
                        if do.eng == e and (e == "pe" or idx_in_eng[o.id] - idx_in_eng[d] > SAME_ENG_SKIP.get(e, 3)):
                            continue
                        key = ("e", do.eng)
                        val = cnt[d]
                    if waited.get(key, 0) >= val:
                        continue
                    if need.get(key, 0) < val:
                        need[key] = val
                for key, val in need.items():
                    waited[key] = val
                    sh = self.dma_sems[key[1]][0] if key[0] == "d" else sems[key[1]]
                    eh.wait_ge(sh, val)
                    self.n_wait += 1
                ins = o.fn(eh)
                if o.dma_sem is not None:
                    ins.then_inc(self.dma_sems[o.dma_sem][0], 16)
                elif o.sig:
                    ins.then_inc(sems[e], 1)

        with nc.Block() as block:
            @block.tensor
            def _(eh):
                emit_engine("pe", eh)

            @block.scalar
            def _(eh):
                emit_engine("act", eh)

            @block.vector
            def _(eh):
                emit_engine("dve", eh)

            @block.gpsimd
            def _(eh):
                emit_engine("pool", eh)

            @block.sync
            def _(eh):
                emit_engine("sp", eh)
                for nm, (sh, c) in self.dma_sems.items():
                    if c > 0:
                        eh.wait_ge(sh, c)


T = 512
NS = 5
NCH_IN = 43
NCH = 59
ALPHA = (2 * 2) ** 0.25
LN_EPS = 1e-5
RMS_EPS = 1e-6
R_C = 8.0
C_RCW, C_RCB, C_BA, C_BX, C_LAM, C_GCW, C_LNG, C_LNB, C_GNW, C_SINK = 0, 32, 40, 48, 56, 64, 112, 128, 144, 145
NCOL_L = 152


def mmc(N, fp32=False):
    return (0.03 + N * 0.00045) * (4 if fp32 else 1)


def actc(F):
    return 0.22 + F / 1200.0


def dvec(F):
    return 0.13 + F / 960.0


def poolc(F):
    return 0.15 + F / 500.0


DBG = False
GSTOP = 0


class StopBuild(Exception):
    pass


def build_program(SEQ, NL=2, en=(1, 1, 1)):
    NG = SEQ // T
    nc = bass.Bass("TRN2", target_bir_lowering=False)
    xT = nc.dram_tensor("xT", [16, 128, SEQ], F32, kind="ExternalInput").ap()
    wall = nc.dram_tensor("wall", [NL, NCH, 128, 16 * 128], F32, kind="ExternalInput").ap()
    cosT = nc.dram_tensor("cosT", [128, SEQ], F32, kind="ExternalInput").ap()
    sinT = nc.dram_tensor("sinT", [128, SEQ], F32, kind="ExternalInput").ap()
    wab = nc.dram_tensor("wab", [NL, 2, 128, 8 * 128], F32, kind="ExternalInput").ap()
    colsD = nc.dram_tensor("cols", [128, NL * NCOL_L], F32, kind="ExternalInput").ap()
    sm4D = nc.dram_tensor("sm4", [4, NL * 2], F32, kind="ExternalInput").ap()
    cst = nc.dram_tensor("cst", [128, 9 * 128], F32, kind="ExternalInput").ap()
    cst4 = nc.dram_tensor("cst4", [4, 1024], F32, kind="ExternalInput").ap()
    outT = nc.dram_tensor("outT", [16, 128, SEQ], F32, kind="ExternalOutput").ap()
    wbf = nc.dram_tensor("wbf", [NL, NCH, 128, 16 * 128], BF16).ap()

    st = ExitStack()
    P = Prog(nc, st)
    A = P.add
    XF = P.sbuf("XF", 128, 16 * T, F32)
    XB = P.sbuf("XB", 128, 16 * T, BF16)
    YT = P.sbuf("YT", 128, 16 * T, BF16)
    WR = [P.sbuf(f"WR{i}", 128, 2048, BF16) for i in range(NS)]
    for i in range(NS):
        P.dsem(f"w{i}")
    for nm_ in ["c0", "c1", "c2", "c3", "cp", "t0", "t1"] + [f"x{i}" for i in range(16)] + [f"o{i}" for i in range(16)]:
        P.dsem(nm_)
    CST = P.sbuf("CST", 128, 9 * 128, F32)
    CST4 = P.sbuf("CST4", 4, 1024, F32)
    COLS = P.sbuf("COLS", 128, NL * NCOL_L, F32)
    SM4 = P.sbuf("SM4", 4, NL * 2, F32)
    WAB = P.sbuf("WAB", 128, NL * 2 * 1024, BF16)
    IDB = P.sbuf("IDB", 128, 128, BF16)
    ONB = P.sbuf("ONB", 128, 128, BF16)
    ONAB = P.sbuf("ONAB", 128, 256, BF16)
    MSK = P.sbuf("MSK", 128, 1024, BF16)
    NMK = P.sbuf("NMK", 128, 512, BF16)
    DER = P.sbuf("DER", 128, NL * 40, F32)
    NEGA = P.sbuf("NEGA", 4, NL * 2, F32)
    COS = P.sbuf("COS", 128, T, F32)
    SIN = P.sbuf("SIN", 128, T, F32)
    S32 = P.sbuf("S32", 128, 10580, F32)
    S16 = P.sbuf("S16", 128, 19456, BF16)
    KB = [[P.sbuf(f"KB{l}_{i}", 128, 128 + T, BF16) for i in range(2)] for l in range(NL)]
    VAB = [P.sbuf(f"VAB{l}", 128, 5 * 256, BF16) for l in range(NL)]
    RXH = [P.sbuf(f"RXH{l}", 128, 24, F32) for l in range(NL)]
    HST = [P.sbuf(f"HST{l}", 128, 8, F32) for l in range(NL)]
    GXH = [P.sbuf(f"GXH{l}", 128, 36, F32) for l in range(NL)]
    SST = [P.sbuf(f"SST{l}", 128, 512, F32) for l in range(NL)]
    SSB = [P.sbuf(f"SSB{l}", 128, 512, BF16) for l in range(NL)]
    SMALL = P.sbuf("SMALL", 128, 256, F32)
    WBF = Tile("WBF", None, 1, NL * NCH, "dram")
    CVR = Tile("CVR", None, 1, 8, "dram")
    for i in range(8):
        P.dsem(f"cv{i}")
    banks = [P.psum(f"B{i}", 128, 512) for i in range(8)]
    bankb = {}
    rr = {"proj": 0, "misc": 0}

    def pbank():
        rr["proj"] += 1
        return banks[rr["proj"] % 3]

    def mbank():
        rr["misc"] += 1
        return banks[3 + rr["misc"] % 5]

    IDENT = CST.v(0, 128); PERM = CST.v(128, 256); ONESF = CST.v(256, 384); OFFD = CST.v(384, 512)
    LASTSEL = CST.v(512, 640)
    SEL = CST4.v(0, 512); RST = CST4.v(512, 1024)

    A("sp", lambda e: e.dma_start(out=CST.v().ap, in_=cst), writes=[CST.v()], dma="c0", cost=2)
    A("sp", lambda e: e.dma_start(out=CST4.v().ap, in_=cst4), writes=[CST4.v()], dma="c1", cost=2)
    A("sp", lambda e: e.dma_start(out=COLS.v().ap, in_=colsD), writes=[COLS.v()], dma="c2", cost=2)
    A("sp", lambda e: e.dma_start(out=SM4.v().ap, in_=sm4D), writes=[SM4.v()], dma="c3", cost=2)
    A("pool", lambda e: e.dma_start(out=WAB.v().ap.rearrange("p (a f) -> p a f", a=NL * 2),
                                    in_=wab.rearrange("l m p f -> p (l m) f")), writes=[WAB.v()], dma="cp", cost=4)
    A("dve", lambda e: e.tensor_copy(out=IDB.v().ap, in_=IDENT.ap), reads=[IDENT], writes=[IDB.v()])
    A("dve", lambda e: e.tensor_copy(out=ONB.v().ap, in_=ONESF.ap), reads=[ONESF], writes=[ONB.v()])
    A("dve", lambda e: e.memset(ONAB.v().ap, 0.0), writes=[ONAB.v()])
    A("dve", lambda e: e.memset(ONAB.v(0, 64).ap, 1.0), writes=[ONAB.v(0, 64)])
    A("dve", lambda e: e.memset(ONAB.v(192, 256).ap, 1.0), writes=[ONAB.v(192, 256)])
    for i in range(2):
        A("dve", lambda e, i=i: e.tensor_copy(out=MSK.v(i * 256, i * 256 + 256).ap, in_=CST.v(640, 896).ap), reads=[CST.v(640, 896)], writes=[MSK.v(i * 256, i * 256 + 256)])
        A("dve", lambda e, i=i: e.tensor_copy(out=MSK.v(512 + i * 256 + 128, 512 + i * 256 + 256).ap, in_=CST.v(768, 896).ap), reads=[CST.v(768, 896)], writes=[MSK.v(512 + i * 256 + 128, 512 + i * 256 + 256)])
        A("dve", lambda e, i=i: e.memset(MSK.v(512 + i * 256, 512 + i * 256 + 128).ap, 0.0), writes=[MSK.v(512 + i * 256, 512 + i * 256 + 128)])
    for i in range(4):
        A("dve", lambda e, i=i: e.tensor_copy(out=NMK.v(i * 128, i * 128 + 128).ap, in_=CST.v(896, 1024).ap), reads=[CST.v(896, 1024)], writes=[NMK.v(i * 128, i * 128 + 128)])
    for l in range(NL):
        cb = l * NCOL_L
        d0 = l * 40
        lam = COLS.v(cb + C_LAM, cb + C_LAM + 8)
        cn = DER.v(d0, d0 + 8); cn2 = DER.v(d0 + 8, d0 + 16); es = DER.v(d0 + 16, d0 + 20)
        A("act", lambda e, lam=lam, cn=cn: e.activation(out=cn.ap, in_=lam.ap, func=AF.Exp, scale=-1.0), reads=[lam], writes=[cn])
        A("act", lambda e, cn=cn: e.activation(out=cn.ap, in_=cn.ap, func=AF.Ln, bias=1.0), reads=[cn], writes=[cn])
        A("dve", lambda e, cn=cn, cn2=cn2: e.tensor_scalar_mul(out=cn2.ap, in0=cn.ap, scalar1=-R_C), reads=[cn], writes=[cn2])
        A("dve", lambda e, cn=cn: e.tensor_scalar_mul(out=cn.ap, in0=cn.ap, scalar1=-0.5 * R_C), reads=[cn, cn2], writes=[cn])
        hb = DER.v(d0 + 20, d0 + 36)
        bab = COLS.v(cb + C_BA, cb + C_BA + 16)
        A("dve", lambda e, hb=hb, bab=bab: e.tensor_scalar_mul(out=hb.ap, in0=bab.ap, scalar1=0.5), reads=[bab], writes=[hb])
        sk = COLS.v(cb + C_SINK, cb + C_SINK + 4)
        A("act", lambda e, sk=sk, es=es: e.activation(out=es.ap, in_=sk.ap, func=AF.Exp), reads=[sk], writes=[es])
        na = NEGA.v(l * 2, l * 2 + 1)
        A("act", lambda e, na=na, l=l: e.activation(out=na.ap, in_=SM4.v(l * 2, l * 2 + 1).ap, func=AF.Exp), reads=[SM4.v()], writes=[na])
        A("dve", lambda e, na=na: e.tensor_scalar_mul(out=na.ap, in0=na.ap, scalar1=-1.0), reads=[na], writes=[na])
        for tl in (KB[l][0], KB[l][1], VAB[l]):
            A("pool", lambda e, tl=tl: e.memset(tl.v().ap, 0.0), writes=[tl.v()])
        for tl in (RXH[l], HST[l], GXH[l], SST[l]):
            A("pool", lambda e, tl=tl: e.memset(tl.v().ap, 0.0), writes=[tl.v()])
        A("pool", lambda e, l=l: e.memset(SSB[l].v().ap, 0.0), writes=[SSB[l].v()])

    A("pool", lambda e: e.memset(S16.v(30 * T, 34 * T).ap, 0.0), writes=[S16.v(30 * T, 34 * T)], cost=2)
    wlist = []
    for g_ in range(NG):
        for l_ in range(NL):
            chs = []
            if en[0]:
                chs += list(range(0, 10))
            if en[2]:
                chs += list(range(26, 43))
            if en[1]:
                chs += list(range(10, 26))
            chs += list(range(43, 59))
            wlist += [(l_, c_) for c_ in chs]
    wstate = {"issued": 0, "j": 0}
    cvj = 0
    for l_ in range(NL):
        for c_ in range(NCH):
            i_ = l_ * NCH + c_
            A("pool", lambda e, l_=l_, c_=c_: e.dma_start(out=wbf[l_, c_], in_=wall[l_, c_]),
              writes=[WBF.v(i_, i_ + 1), CVR.v(cvj % 8, cvj % 8 + 1)], dma=f"cv{cvj % 8}", cost=9.0, name=f"cv{i_}")
            cvj += 1

    def next_w():
        j = wstate["j"]
        wstate["j"] += 1
        while wstate["issued"] < min(j + NS, len(wlist)):
            i = wstate["issued"]
            lyr, ch = wlist[i]
            s = i % NS
            A("sp", lambda e, s=s, lyr=lyr, ch=ch: e.dma_start(out=WR[s].v().ap, in_=wbf[lyr, ch]),
              reads=[WBF.v(lyr * NCH + ch, lyr * NCH + ch + 1)], writes=[WR[s].v()], dma=f"w{s}", cost=2.5, name=f"w{i}")
            wstate["issued"] += 1
        return WR[j % NS]

    pend = []

    def later(fn):
        pend.append(fn)

    def flush():
        while pend:
            pend.pop(0)()

    def proj(bank, M=128, coff=0, w=None, do_flush=True):
        if w is None:
            w = next_w()
        out = bank.v(0, T, 0, M)
        for k in range(16):
            lh = w.v(k * 128 + coff, k * 128 + coff + M)
            rh = XB.v(k * T, (k + 1) * T)
            A("pe", lambda e, out=out, lh=lh, rh=rh, k=k: e.matmul(out.ap, lhsT=lh.ap, rhs=rh.ap, start=(k == 0), stop=(k == 15)),
              reads=[lh, rh], writes=[out], cost=mmc(T))
        if do_flush:
            flush()
        return w

    def proj_pieces(post, npiece=4):
        stt_ = {}
        per = 16 // npiece

        def piece(i):
            if i == 0:
                stt_["bk"] = pbank()
                stt_["w"] = next_w()
            bk, w = stt_["bk"], stt_["w"]
            out = bk.v(0, T)
            for k in range(i * per, (i + 1) * per):
                lh = w.v(k * 128, k * 128 + 128)
                rh = XB.v(k * T, (k + 1) * T)
                A("pe", lambda e, out=out, lh=lh, rh=rh, k=k: e.matmul(out.ap, lhsT=lh.ap, rhs=rh.ap, start=(k == 0), stop=(k == 15)),
                  reads=[lh, rh], writes=[out], cost=mmc(T))
            if i == npiece - 1:
                flush()
                post(bk)
        return [lambda i=i: piece(i) for i in range(npiece)]

    def act(out, in_, func, cost=None, reads=None, **kw):
        if func == AF.Copy and kw:
            func = AF.Identity
        rd = [in_] + list(reads or [])
        A("act", lambda e: e.activation(out=out.ap, in_=in_.ap, func=func, **kw), reads=rd, writes=[out],
          cost=cost or actc(out.f1 - out.f0))

    def lnexp(out, in_, k, reads=None, **kw):
        act(out, in_, AF.Ln, reads=reads, **kw)
        act(out, out, AF.Exp, scale=k)

    def tt(eng, out, a, b, op, cost=None):
        c = cost or (dvec(out.f1 - out.f0) if eng == "dve" else poolc(out.f1 - out.f0))
        A(eng, lambda e: e.tensor_tensor(out=out.ap, in0=a.ap, in1=b.ap, op=op), reads=[a, b], writes=[out], cost=c)

    def ts(eng, out, a, s1, s2, op0, op1=None, reads=(), cost=None):
        c = cost or (dvec(out.f1 - out.f0) if eng == "dve" else poolc(out.f1 - out.f0))
        s1a = s1.ap if isinstance(s1, View) else s1
        s2a = s2.ap if isinstance(s2, View) else s2
        rd = [a] + [s_ for s_ in (s1, s2) if isinstance(s_, View)] + list(reads)
        if s2 is None:
            A(eng, lambda e: e.tensor_scalar(out=out.ap, in0=a.ap, scalar1=s1a, scalar2=None, op0=op0), reads=rd, writes=[out], cost=c)
        else:
            A(eng, lambda e: e.tensor_scalar(out=out.ap, in0=a.ap, scalar1=s1a, scalar2=s2a, op0=op0, op1=op1), reads=rd, writes=[out], cost=c)

    def stt(eng, out, a, s, b, op0, op1, cost=None):
        eng = "dve"
        c = cost or (dvec(out.f1 - out.f0) if eng == "dve" else poolc(out.f1 - out.f0))
        sa = s.ap if isinstance(s, View) else s
        rd = [a, b] + ([s] if isinstance(s, View) else [])
        A(eng, lambda e: e.scalar_tensor_tensor(out=out.ap, in0=a.ap, scalar=sa, in1=b.ap, op0=op0, op1=op1), reads=rd, writes=[out], cost=c)

    def mm(out, lh, rh, start=True, stop=True, fp32=False):
        A("pe", lambda e: e.matmul(out.ap, lhsT=lh.ap, rhs=rh.ap, start=start, stop=stop), reads=[lh, rh], writes=[out],
          cost=mmc(rh.f1 - rh.f0, fp32))

    def tr(out, in_, ident):
        A("pe", lambda e: e.transpose(out.ap, in_.ap, ident.ap), reads=[in_, ident], writes=[out], cost=0.12)

    def cp(eng, out, in_, cost=None):
        c = cost or (dvec(out.f1 - out.f0) if eng == "dve" else poolc(out.f1 - out.f0) if eng == "pool" else actc(out.f1 - out.f0))
        if eng == "act":
            A("act", lambda e: e.activation(out=out.ap, in_=in_.ap, func=AF.Copy), reads=[in_], writes=[out], cost=c)
        else:
            A(eng, lambda e: e.tensor_copy(out=out.ap, in_=in_.ap), reads=[in_], writes=[out], cost=c)

    def gstop(k, dumps=()):
        if GSTOP != k:
            return
        off = 0
        for v_ in dumps:
            w_ = v_.f1 - v_.f0
            dst_ = View(XF, v_.p0, v_.p1, off, off + w_)
            cp("dve", dst_, v_)
            off += w_
        A("sp", lambda e: e.dma_start(out=outT[:, :, 0:T].rearrange("m p t -> p m t"),
                                      in_=XF.v().ap.rearrange("p (m t) -> p m t", m=16)),
          reads=[XF.v()], dma="o0", cost=15.0)
        raise StopBuild()

    try:
        for g in range(NG):
            t0 = g * T
            for m_ in range(16):
                A("sp", lambda e, t0=t0, m_=m_: e.dma_start(out=XF.v(m_ * T, (m_ + 1) * T).ap, in_=xT[m_, :, t0:t0 + T]),
                  writes=[XF.v(m_ * T, (m_ + 1) * T)], dma=f"x{m_}", cost=1.5)
            A("sp", lambda e, t0=t0: e.dma_start(out=COS.v().ap, in_=cosT[:, t0:t0 + T]), writes=[COS.v()], dma="t0", cost=2.0)
            A("sp", lambda e, t0=t0: e.dma_start(out=SIN.v().ap, in_=sinT[:, t0:t0 + T]), writes=[SIN.v()], dma="t1", cost=2.0)
            for m_ in range(16):
                cp(("pool", "act", "dve", "act")[m_ % 4], XB.v(m_ * T, (m_ + 1) * T), XF.v(m_ * T, (m_ + 1) * T))
            for l in range(NL):
                cb = l * NCOL_L
                col = lambda c0, n=1: COLS.v(cb + c0, cb + c0 + n)
                d0 = l * 40
                qfs = [S32.v(i * 3 * T, i * 3 * T + T) for i in range(2)]
                t1s = [S32.v(i * 3 * T + T, i * 3 * T + 2 * T) for i in range(2)]
                t2s = [S32.v(i * 3 * T + 2 * T, i * 3 * T + 3 * T) for i in range(2)]
                qT = [S16.v((20 + c) * T, (21 + c) * T) for c in range(4)]
                gA = [S16.v((24 + c) * T, (25 + c) * T) for c in range(4)]
                vf = S16.v(28 * T, 29 * T)
                PT = [S16.v((29 + i) * T, (30 + i) * T) for i in range(2)]
                ysm = S32.v(10324, 10324 + 128); rdn = S32.v(10324 + 128, 10324 + 256)
                AT_steps = []

                def rope(bk, dsts, i):
                    qf, t1, t2 = qfs[i % 2], t1s[i % 2], t2s[i % 2]
                    act(qf, bk.v(0, T), AF.Copy)

                    def rest():
                        pb = mbank()
                        mm(pb.v(0, T), PERM, qf, fp32=True)
                        tt("pool", t1, qf, COS.v(), ALU.mult)
                        tt("dve", t2, pb.v(0, T), SIN.v(), ALU.mult)
                        for dst in dsts:
                            a1_ = View(t1.tile, dst.p0, dst.p1, t1.f0, t1.f1)
                            a2_ = View(t2.tile, dst.p0, dst.p1, t2.f0, t2.f1)
                            tt("dve", dst, a1_, a2_, ALU.add)
                    later(rest)

                if en[0]:
                    for c in range(4):
                        bk = pbank(); proj(bk)
                        rope(bk, [qT[c]], c)
                    bk = pbank(); proj(bk)
                    rope(bk, [KB[l][0].v(128, 128 + T, 0, 64), KB[l][1].v(128, 128 + T, 64, 128)], 4)
                    bk = pbank(); proj(bk)
                    act(vf, bk.v(0, T), AF.Copy)

                    def vtr():
                        for b in range(4):
                            pb = mbank()
                            mm(pb.v(0, 128), vf.sub(b * 128, b * 128 + 128), IDB.v())
                            s_ = (b + 1) * 256
                            cp("dve", VAB[l].v(s_, s_ + 64), pb.v(0, 64), cost=0.2)
                            cp("dve", VAB[l].v(s_ + 192, s_ + 256), pb.v(64, 128), cost=0.2)
                    later(vtr)
                    for c in range(4):
                        bk = pbank(); proj(bk)
                        tg = t1s[c % 2] if False else S32.v(6 * T + 256 + (c % 2) * T, 6 * T + 256 + (c % 2) * T + T)
                        act(tg, bk.v(0, T), AF.Tanh, scale=0.5)
                        stt("dve", gA[c], tg, 1.0, bk.v(0, T), ALU.add, ALU.mult)
                    flush()

                    def s_part(b, c, i):
                        first = (g == 0 and b == 0)
                        pa = mbank()
                        for hh in range(2):
                            for kb_ in range(2):
                                out = pa.v((hh * 2 + kb_) * 128, (hh * 2 + kb_) * 128 + 128)
                                lh = KB[l][hh].v((b + kb_) * 128, (b + kb_) * 128 + 128)
                                rh = qT[c].sub(b * 128, b * 128 + 128)
                                mm(out, lh, rh)
                        pt = PT[i % 2]
                        act(pt, pa.v(0, T), AF.Exp, scale=0.125)
                        mk = MSK.v(512, 1024) if first else MSK.v(0, 512)
                        tt("dve", pt, pt, mk, ALU.mult, cost=0.4)
                        return pt

                    def nd_part(b, c, pt):
                        pn = mbank()
                        for i, (hh, kb_) in enumerate(((0, 0), (0, 1), (1, 0), (1, 1))):
                            s_ = (b + kb_) * 256 + hh * 128
                            mm(pn.v(0, 128), VAB[l].v(s_, s_ + 128), pt.sub((hh * 2 + kb_) * 128, (hh * 2 + kb_) * 128 + 128), start=(i == 0), stop=(i == 3))
                        for i, (hh, kb_) in enumerate(((0, 0), (0, 1), (1, 0), (1, 1))):
                            mm(pn.v(128, 256), ONAB.v(hh * 128, hh * 128 + 128), pt.sub((hh * 2 + kb_) * 128, (hh * 2 + kb_) * 128 + 128), start=(i == 0), stop=(i == 3))
                        es_ = DER.v(d0 + 16 + c, d0 + 17 + c)
                        lnexp(rdn, pn.v(128, 256), -1.0, reads=[es_], bias=es_.ap, cost=0.33)
                        tt("dve", ysm, pn.v(0, 128), rdn, ALU.mult, cost=0.3)
                        yo = YT.v(c * T + b * 128, c * T + b * 128 + 128)
                        stt("dve", yo, ysm, 0.5, gA[c].sub(b * 128, b * 128 + 128), ALU.mult, ALU.mult, cost=0.3)

                    its_ = [(b, c) for b in range(4) for c in range(4)]
                    ptl = {}

                    def at_s(i):
                        ptl[i] = s_part(its_[i][0], its_[i][1], i)

                    def at_nd(i):
                        nd_part(its_[i][0], its_[i][1], ptl[i])
                    for i in range(16):
                        AT_steps.append(lambda i=i: at_s(i))
                        if i > 0:
                            AT_steps.append(lambda i=i: at_nd(i - 1))
                    AT_steps.append(lambda: at_nd(15))

                    def at_roll():
                        cp("pool", KB[l][0].v(0, 128, 0, 64), KB[l][0].v(T, T + 128, 0, 64), cost=0.3)
                        cp("pool", KB[l][1].v(0, 128, 64, 128), KB[l][1].v(T, T + 128, 64, 128), cost=0.3)
                        cp("pool", VAB[l].v(0, 256), VAB[l].v(1024, 1280), cost=0.4)
                    AT_steps.append(at_roll)
                else:
                    A("pool", lambda e: e.memset(YT.v(0, 4 * T).ap, 0.0), writes=[YT.v(0, 4 * T)], cost=2)
                R_steps = []
                if en[1]:
                    RXB = S32.v(0, 515); xr = S32.v(520, 520 + T); r_ = S32.v(1040, 1040 + T); tg = S32.v(1560, 1560 + T)
                    a_ = S32.v(2600, 2600 + T); a2 = S32.v(3120, 3120 + T); ig = S32.v(3640, 3640 + T); hh_ = S32.v(7760, 7760 + T)
                    xrb = S16.v(36 * T, 37 * T); gz = S16.v(37 * T, 38 * T)

                    def r_post_a(n, bk):
                        cp("pool", RXB.sub(0, 3), RXH[l].v(3 * n, 3 * n + 3), cost=0.15)
                        act(RXB.sub(3, 515), bk.v(0, T), AF.Copy)
                        cp("pool", RXH[l].v(3 * n, 3 * n + 3), RXB.sub(512, 515), cost=0.15)
                        ts("pool", xr, RXB.sub(3, 515), col(C_RCW + 4 * n + 3), col(C_RCB + n), ALU.mult, ALU.add)
                        for k in range(3):
                            stt("dve", xr, RXB.sub(k, k + T), col(C_RCW + 4 * n + k), xr, ALU.mult, ALU.add)
                        cp("act", xrb, xr)

                        def gates():
                            wa = WAB.v((l * 2) * 1024 + n * 128, (l * 2) * 1024 + n * 128 + 128)
                            wx = WAB.v((l * 2 + 1) * 1024 + n * 128, (l * 2 + 1) * 1024 + n * 128 + 128)
                            pga = mbank(); mm(pga.v(0, T), wa, xrb)
                            pgx = mbank(); mm(pgx.v(0, T), wx, xrb)
                            hba = DER.v(d0 + 20 + n, d0 + 21 + n); hbx = DER.v(d0 + 28 + n, d0 + 29 + n)
                            cnh = DER.v(d0 + n, d0 + n + 1); cn_ = DER.v(d0 + 8 + n, d0 + 9 + n)
                            act(r_, pga.v(0, T), AF.Tanh, scale=0.5, bias=hba.ap, reads=[hba])
                            act(ig, pgx.v(0, T), AF.Tanh, scale=0.5, bias=hbx.ap, reads=[hbx])
                            act(a_, r_, AF.Exp, scale=cnh.ap, bias=cnh.ap, reads=[cnh])
                            act(a2, r_, AF.Exp, scale=cn_.ap, bias=cn_.ap, reads=[cn_])
                            act(a2, a2, AF.Ln, scale=-1.0, bias=1.0)
                            act(a2, a2, AF.Exp, scale=0.5, bias=math.log(0.5))
                            stt("dve", ig, ig, 1.0, xr, ALU.add, ALU.mult)
                            tt("pool", ig, ig, a2, ALU.mult)
                            hs_ = HST[l].v(n, n + 1)
                            A("dve", lambda e: e.tensor_tensor_scan(out=hh_.ap, data0=a_.ap, data1=ig.ap, initial=hs_.ap, op0=ALU.mult, op1=ALU.add),
                              reads=[a_, ig, hs_], writes=[hh_], cost=dvec(T))
                            cp("pool", hs_, hh_.sub(T - 1, T), cost=0.15)
                        later(gates)

                    def r_post_b(n, bk):
                        act(tg, bk.v(0, T), AF.Tanh, scale=0.5)
                        stt("dve", gz, tg, 1.0, bk.v(0, T), ALU.add, ALU.mult)

                        def fin():
                            stt("dve", YT.v((4 + n) * T, (5 + n) * T), hh_, 0.5, gz, ALU.mult, ALU.mult)
                        later(fin)

                    for n in range(8):
                        R_steps += proj_pieces(lambda bk, n=n: r_post_a(n, bk))
                        R_steps += proj_pieces(lambda bk, n=n: r_post_b(n, bk))
                else:
                    A("pool", lambda e: e.memset(YT.v(4 * T, 12 * T).ap, 0.0), writes=[YT.v(4 * T, 12 * T)], cost=4)
                if en[2]:
                    GXB = S32.v(0, 515); cqkv = [S32.v(520 + i * 520, 520 + i * 520 + T) for i in range(3)]
                    sq = S32.v(2080, 2080 + T); rn = S32.v(2600, 2600 + T)
                    EGt = [S32.v(3120 + i * 520, 3120 + i * 520 + T) for i in range(2)]
                    DT = [S32.v(4160 + i * 520, 4160 + i * 520 + T) for i in range(2)]
                    BV = [S32.v(5200 + i * 512, 5200 + i * 512 + T) for i in range(4)]
                    DIAGB = S32.v(7248, 7248 + T)
                    tgA = S32.v(9300, 9300 + T); tgB = S32.v(9812, 9812 + T)
                    BETA = S32.v(7760, 7760 + T, 0, 4); GG = S32.v(8272, 8272 + T, 0, 4); GCUM = S32.v(8784, 8784 + T, 0, 4)
                    qn = [S16.v(i * T, (i + 1) * T) for i in range(4)]
                    kn = [S16.v((4 + i) * T, (5 + i) * T) for i in range(4)]
                    QD = [S16.v((8 + i) * T, (9 + i) * T) for i in range(4)]
                    gG = [S16.v((12 + i) * T, (13 + i) * T) for i in range(4)]
                    OT = [S16.v((16 + i) * T, (17 + i) * T) for i in range(4)]
                    KTOK = [S16.v((20 + i) * T, (21 + i) * T) for i in range(2)]
                    Qr = [S16.v((22 + i) * T, (23 + i) * T) for i in range(2)]
                    QTr = [S16.v((24 + i) * T, (25 + i) * T) for i in range(2)]
                    Xr = [S16.v((26 + i) * T, (27 + i) * T) for i in range(2)]
                    KQm = [S16.v((28 + i) * T, (29 + i) * T) for i in range(2)]
                    Rb = S16.v(30 * T, 31 * T); VN = S16.v(31 * T, 32 * T); VNS2 = [S16.v((32 + i) * T, (33 + i) * T) for i in range(2)]
                    XFIN = [S16.v((34 + i) * T, (35 + i) * T) for i in range(2)]
                    hs = lambda v, h, r0=0, r1=128: View(v.tile, r0, r1, v.f0 + h * 128, v.f0 + h * 128 + 128)
                    GCBC = SMALL.v(0, 32); NEGGC = SMALL.v(32, 64); EGC = SMALL.v(64, 96); NBEG = SMALL.v(96, 112)
                    EDL = SMALL.v(112, 144); GLC = SMALL.v(144, 176)
                    GH_steps = []

                    def g_bg():
                        w = next_w()
                        bkb = mbank(); proj(bkb, M=4, coff=0, w=w, do_flush=False)
                        bka = mbank(); proj(bka, M=4, coff=32, w=w)
                        act(BETA, bkb.v(0, T, 0, 4), AF.Tanh, scale=0.5)
                        ts("dve", BETA, BETA, 0.5, 0.5, ALU.mult, ALU.add)
                        dtb = SM4.v(l * 2 + 1, l * 2 + 2)
                        act(GG, bka.v(0, T, 0, 4), AF.Exp, bias=dtb.ap, reads=[dtb])
                        act(GG, GG, AF.Ln, bias=1.0)
                        ts("dve", GG, GG, NEGA.v(l * 2, l * 2 + 1), None, ALU.mult)
                        A("dve", lambda e: e.tensor_tensor_scan(out=GCUM.ap, data0=RST.ap, data1=GG.ap, initial=0.0, op0=ALU.mult, op1=ALU.add),
                          reads=[RST, GG], writes=[GCUM], cost=dvec(T))

                        def colforms():
                            pc_ = mbank()
                            for p in range(4):
                                mm(pc_.v(p * 8, p * 8 + 4), GCUM.sub(p * 128, p * 128 + 128), CST.v(0, 4, 0, 4), fp32=True)
                                mm(pc_.v(p * 8 + 4, p * 8 + 8), BETA.sub(p * 128, p * 128 + 128), CST.v(0, 4, 0, 4), fp32=True)
                            cp("dve", GCBC, pc_.v(0, 32), cost=0.2)
                            ts("dve", NEGGC, GCBC, -1.0, None, ALU.mult, cost=0.2)
                            act(EGC, GCBC, AF.Exp, cost=0.3)
                            for p in range(4):
                                stt("dve", NBEG.sub(p * 4, p * 4 + 4), GCBC.sub(p * 8 + 4, p * 8 + 8), -1.0, EGC.sub(p * 8, p * 8 + 4), ALU.mult, ALU.mult, cost=0.2)
                            pl_ = mbank()
                            mm(pl_.v(0, 32), LASTSEL, GCBC, fp32=True)
                            tt("dve", EDL, pl_.v(0, 32), GCBC, ALU.subtract, cost=0.2)
                            act(EDL, EDL, AF.Exp, cost=0.3)
                        later(colforms)
                    GH_steps.append(g_bg)

                    def g_conv_post(h, qi, bk):
                        dst = cqkv[qi]
                        ci = qi * 4 + h
                        cp("pool", GXB.sub(0, 3), GXH[l].v(3 * ci, 3 * ci + 3), cost=0.15)
                        act(GXB.sub(3, 515), bk.v(0, T), AF.Copy)
                        cp("pool", GXH[l].v(3 * ci, 3 * ci + 3), GXB.sub(512, 515), cost=0.15)
                        ts("pool", dst, GXB.sub(3, 515), col(C_GCW + 4 * ci + 3), 0.0, ALU.mult, ALU.add)
                        for k in range(3):
                            stt("dve", dst, GXB.sub(k, k + T), col(C_GCW + 4 * ci + k), dst, ALU.mult, ALU.add)
                        act(tgA, dst, AF.Tanh, scale=0.5)
                        stt("dve", dst, tgA, 1.0, dst, ALU.add, ALU.mult)
                        if qi == 2:
                            later(lambda: headpost(h))

                    def headpost(h):
                        for src, dstb, scl in ((cqkv[0], qn[h], 0.5 * 128 ** -0.5), (cqkv[1], kn[h], 0.5)):
                            act(sq, src, AF.Square, scale=0.5)
                            pb = mbank(); mm(pb.v(0, T), ONESF, sq, fp32=True)
                            lnexp(rn, pb.v(0, T), -0.5, bias=RMS_EPS)
                            stt("dve", dstb, src, scl, rn, ALU.mult, ALU.mult)
                        pv = mbank()
                        for p in range(4):
                            mm(pv.v(p * 128, p * 128 + 128), cqkv[2].sub(p * 128, p * 128 + 128), IDENT, fp32=True)
                        for p in range(4):
                            bc = GCBC.sub(p * 8 + 4 + h, p * 8 + 5 + h)
                            ts("dve", hs(BV[p], h), pv.v(p * 128, p * 128 + 128), bc, 0.5, ALU.mult, ALU.mult, cost=0.3)
                        pg = mbank()
                        mm(pg.v(0, T), SEL.sub(h * 128, h * 128 + 128), GCUM, fp32=True)
                        eg = EGt[h % 2]
                        act(eg, pg.v(0, T), AF.Exp)
                        tt("dve", QD[h], qn[h], eg, ALU.mult)
                        glv = GLC.sub(h * 8, h * 8 + 8)
                        A("pool", lambda e, eg=eg, glv=glv: e.tensor_copy(out=glv.ap, in_=eg.ap.rearrange("p (c t) -> p c t", t=64)[:, :, 63]),
                          reads=[eg], writes=[glv], cost=0.2)

                    def g_gate_post(h, bk):
                        act(tgB, bk.v(0, T), AF.Tanh, scale=0.5)
                        stt("dve", gG[h], tgB, 1.0, bk.v(0, T), ALU.add, ALU.mult)

                    for h in range(4):
                        for qi in range(3):
                            GH_steps += proj_pieces(lambda bk, h=h, qi=qi: g_conv_post(h, qi, bk))
                        GH_steps += proj_pieces(lambda bk, h=h: g_gate_post(h, bk))
                    ai = 0
                    for gi, gs in enumerate(GH_steps):
                        gs()
                        if gi % 2 == 0 and ai < len(AT_steps):
                            AT_steps[ai](); ai += 1
                    while ai < len(AT_steps):
                        AT_steps[ai](); ai += 1
                    flush()
                    GP_steps = []

                    def pair_steps(p):
                        dt_ = DT[p % 2]

                        def s1():
                            pm = mbank()
                            for h in range(4):
                                mm(hs(pm.v(0, T), h), SEL.sub(h * 128, h * 128 + 128), GCUM.sub(p * 128, p * 128 + 128), start=True, stop=False, fp32=True)
                                mm(hs(pm.v(0, T), h), IDB.v(), NMK.v(0, 128), start=False, stop=True)
                                ng = NEGGC.sub(p * 8 + h, p * 8 + h + 1)
                                act(hs(dt_, h), hs(pm.v(0, T), h), AF.Exp, bias=ng.ap, reads=[ng], cost=0.35)
                        GP_steps.append(s1)

                        def s2():
                            pKK = mbank(); pKQ = mbank(); pBR = mbank()
                            for h in range(4):
                                kp = kn[h].sub(p * 128, p * 128 + 128)
                                mm(hs(pKK.v(0, T), h), kp, kp)
                                mm(hs(pKQ.v(0, T), h), kp, qn[h].sub(p * 128, p * 128 + 128))
                            bcs = GCBC.sub(p * 8 + 4, p * 8 + 8)
                            A("dve", lambda e, bcs=bcs: e.tensor_tensor(out=DIAGB.r3(4), in0=IDENT.ap.unsqueeze(1).to_broadcast([128, 4, 128]),
                                                                       in1=bcs.ap.unsqueeze(2).to_broadcast([128, 4, 128]), op=ALU.mult),
                              reads=[IDENT, bcs], writes=[DIAGB], cost=dvec(T))
                            for h in range(4):
                                mm(hs(pBR.v(0, T), h), OFFD, hs(DIAGB, h), fp32=True)
                            tt("dve", sq, pKK.v(0, T), dt_, ALU.mult)
                            tt("dve", Qr[0], sq, pBR.v(0, T), ALU.mult)
                            tt("dve", KQm[p % 2], pKQ.v(0, T), dt_, ALU.mult)
                            A("pool", lambda e: e.tensor_tensor(out=Xr[0].r3(4), in0=Qr[0].r3(4), in1=IDB.v().ap.unsqueeze(1).to_broadcast([128, 4, 128]), op=ALU.add),
                              reads=[Qr[0], IDB.v()], writes=[Xr[0]], cost=poolc(T))
                        GP_steps.append(s2)

                        def s3():
                            pt_ = mbank()
                            for h in range(4):
                                mm(hs(pt_.v(0, T), h), hs(Qr[0], h), IDB.v())
                            cp("act", QTr[0], pt_.v(0, T))
                            pk_ = mbank()
                            for h in range(4):
                                mm(hs(pk_.v(0, T), h), kn[h].sub(p * 128, p * 128 + 128), IDB.v())
                            cp("act", KTOK[p % 2], pk_.v(0, T))
                        GP_steps.append(s3)

                        def lvl(k):
                            Qp, QpT = Qr[(k - 1) % 2], QTr[(k - 1) % 2]
                            Qn_, QnT = Qr[k % 2], QTr[k % 2]
                            Xp = Xr[(k - 1) % 2]
                            Xn_ = XFIN[p % 2] if k == 5 else Xr[k % 2]
                            pa_ = mbank()
                            for h in range(4):
                                mm(hs(pa_.v(0, T), h), hs(Qp, h), hs(QpT, h))
                            cp("act", QnT, pa_.v(0, T))
                            if k < 5:
                                pb_ = mbank()
                                for h in range(4):
                                    mm(hs(pb_.v(0, T), h), hs(QpT, h), hs(Qp, h))
                                cp("dve", Qn_, pb_.v(0, T))
                            px_ = mbank()
                            for h in range(4):
                                mm(hs(px_.v(0, T), h), hs(QnT, h), hs(Xp, h))
                            tt("dve", Xn_, px_.v(0, T), Xp, ALU.add)
                        for k in range(1, 6):
                            GP_steps.append(lambda k=k: lvl(k))

                        def chunk_a(cc):
                            r0 = (cc % 2) * 64; r1 = r0 + 64; lc = r0; gc0 = cc * 64
                            pKS = mbank()
                            for h in range(4):
                                mm(hs(pKS.v(0, T), h, r0, r1), kn[h].sub(gc0, gc0 + 64), hs(SSB[l].v(), h))
                            for h in range(4):
                                nb = View(SMALL, r0, r1, 96 + p * 4 + h, 96 + p * 4 + h + 1)
                                stt("dve", hs(Rb, h, r0, r1), hs(pKS.v(0, T), h, r0, r1), nb, hs(BV[p], h, r0, r1), ALU.mult, ALU.add, cost=0.3)
                            pVN = mbank()
                            for h in range(4):
                                lh = XFIN[p % 2].sub(h * 128 + lc, h * 128 + lc + 64)
                                mm(hs(pVN.v(0, T), h, r0, r1), lh, hs(Rb, h))
                            cp("act", View(VN.tile, r0, r1, VN.f0, VN.f1), pVN.v(0, T, r0, r1))
                            VNS = VNS2[cc % 2]
                            for h in range(4):
                                ed = View(SMALL, r0, r1, 112 + p * 8 + h, 112 + p * 8 + h + 1)
                                ts("dve", hs(VNS, h, r0, r1), hs(pVN.v(0, T), h, r0, r1), ed, None, ALU.mult, cost=0.3)

                        def chunk_b(cc):
                            r0 = (cc % 2) * 64; r1 = r0 + 64; lc = r0; gc0 = cc * 64
                            VNS = VNS2[cc % 2]
                            pO = mbank()
                            for h in range(4):
                                o_ = pO.v(h * 64, h * 64 + 64)
                                mm(o_, hs(SSB[l].v(), h), QD[h].sub(gc0, gc0 + 64), start=True, stop=False)
                                rh = KQm[p % 2].sub(h * 128 + lc, h * 128 + lc + 64)
                                mm(o_, hs(VN, h), rh, start=False, stop=True)
                            for h in range(4):
                                cp("act" if h % 2 else "dve", OT[h].sub(gc0, gc0 + 64), pO.v(h * 64, h * 64 + 64), cost=0.25)
                            pDS = mbank()
                            for h in range(4):
                                mm(hs(pDS.v(0, T), h), hs(KTOK[p % 2], h), hs(VNS, h))
                            for h in range(4):
                                gl = GLC.sub(h * 8 + cc, h * 8 + cc + 1)
                                stt("dve", hs(SST[l].v(), h), hs(SST[l].v(), h), gl, hs(pDS.v(0, T), h), ALU.mult, ALU.add, cost=0.3)
                            cp("act", SSB[l].v(), SST[l].v())
                        for cc in (2 * p, 2 * p + 1):
                            GP_steps.append(lambda cc=cc: chunk_a(cc))
                            GP_steps.append(lambda cc=cc: chunk_b(cc))

                    for p in range(4):
                        pair_steps(p)
                    ri = 0
                    for gi, gs in enumerate(GP_steps):
                        for _ in range(2 if gi % 3 == 0 else 1):
                            if ri < len(R_steps):
                                R_steps[ri](); ri += 1
                        gs()
                    while ri < len(R_steps):
                        R_steps[ri](); ri += 1
                    flush()
                    gstop(4, [OT[0], OT[1], SST[l].v()])
                    for h in range(4):
                        act(sq, OT[h], AF.Square)
                        pb = mbank(); mm(pb.v(0, T), ONESF, sq, fp32=True)
                        lnexp(rn, pb.v(0, T), -0.5, scale=1.0 / 128, bias=RMS_EPS)
                        stt("dve", cqkv[0], OT[h], 0.5, rn, ALU.mult, ALU.mult)
                        stt("dve", YT.v((12 + h) * T, (13 + h) * T), cqkv[0], col(C_GNW), gG[h], ALU.mult, ALU.mult)
                else:
                    A("pool", lambda e: e.memset(YT.v(12 * T, 16 * T).ap, 0.0), writes=[YT.v(12 * T, 16 * T)], cost=2)
                    for as_ in AT_steps:
                        as_()
                    for rs_ in R_steps:
                        rs_()
                    flush()
                if DBG:
                    cp("dve", XF.v(0, 4 * T), YT.v(0, 4 * T))
                    cp("dve", XF.v(4 * T, 5 * T), S16.v(0, T))
                    cp("dve", XF.v(5 * T, 6 * T), KB[l][0].v(128, 128 + T))
                    cp("dve", XF.v(6 * T, 7 * T), S16.v(4 * T, 5 * T))
                    cp("dve", XF.v(7 * T, 8 * T), S16.v(9 * T, 10 * T))
                    A("sp", lambda e, t0=t0: e.dma_start(out=outT[:, :, t0:t0 + T].rearrange("m p t -> p m t"),
                                                         in_=XF.v().ap.rearrange("p (m t) -> p m t", m=16)),
                      reads=[XF.v()], dma="o0", cost=15.0)
                    return nc, P, st
                ps1 = mbank(); ps2 = mbank()

                def stats(m):
                    zbm = S16.v((2 * (m % 2)) * T, (2 * (m % 2) + 1) * T); zsm = S16.v((2 * (m % 2) + 1) * T, (2 * (m % 2) + 2) * T)
                    mm(ps1.v(0, T), ONB.v(), zbm, start=(m == 0), stop=(m == 15))
                    mm(ps2.v(0, T), ONB.v(), zsm, start=(m == 0), stop=(m == 15))

                for m in range(16):
                    bk = pbank()
                    w = next_w()
                    for k in range(16):
                        lh = w.v(k * 128, k * 128 + 128); rh = YT.v(k * T, (k + 1) * T)
                        A("pe", lambda e, bk=bk, lh=lh, rh=rh, k=k: e.matmul(bk.v(0, T).ap, lhsT=lh.ap, rhs=rh.ap, start=(k == 0), stop=(k == 15)),
                          reads=[lh, rh], writes=[bk.v(0, T)], cost=mmc(T))
                    if m > 0:
                        stats(m - 1)
                    xm = XF.v(m * T, (m + 1) * T)
                    stt("dve", xm, xm, ALPHA, bk.v(0, T), ALU.mult, ALU.add)
                    zbm = S16.v((2 * (m % 2)) * T, (2 * (m % 2) + 1) * T); zsm = S16.v((2 * (m % 2) + 1) * T, (2 * (m % 2) + 2) * T)
                    cp("dve", zbm, xm)
                    act(zsm, xm, AF.Square)
                stats(15)
                mean = S32.v(0, T); msq = S32.v(T, 2 * T); rstd = S32.v(2 * T, 3 * T); mr = S32.v(3 * T, 4 * T)
                tmpa = S32.v(4 * T, 5 * T); tmpb = S32.v(5 * T, 6 * T)
                act(mean, ps1.v(0, T), AF.Copy, scale=1.0 / 2048)
                act(msq, mean, AF.Square)
                stt("dve", rstd, ps2.v(0, T), 1.0 / 2048, msq, ALU.mult, ALU.subtract)
                lnexp(rstd, rstd, -0.5, bias=LN_EPS)
                tt("dve", mr, mean, rstd, ALU.mult)
                last = (l == NL - 1)
                for m in range(16):
                    xm = XF.v(m * T, (m + 1) * T)
                    tm = tmpa if m % 2 == 0 else tmpb
                    tt("dve", tm, xm, rstd, ALU.mult)
                    tt("dve" if m % 4 else "pool", tm, tm, mr, ALU.subtract)
                    lg = col(C_LNG + m); lb = col(C_LNB + m)
                    if not last:
                        act(XB.v(m * T, (m + 1) * T), tm, AF.Identity, scale=lg.ap, bias=lb.ap, reads=[lg, lb])
                        ts("pool", xm, tm, lg, lb, ALU.mult, ALU.add)
                    else:
                        act(xm, tm, AF.Identity, scale=lg.ap, bias=lb.ap, reads=[lg, lb])
                        A("sp", lambda e, t0=t0, m=m, xm=xm: e.dma_start(out=outT[m, :, t0:t0 + T], in_=xm.ap),
                          reads=[xm], dma=f"o{m}", cost=1.5)
    except StopBuild:
        pass
    return nc, P, st


def _chunk_cols():
    ch = []
    def heads(base, c):
        return list(range(base + c * 64, base + c * 64 + 64)) + list(range(base + (4 + c) * 64, base + (4 + c) * 64 + 64))
    for c in range(4):
        ch.append(heads(0, c))
    ch.append(list(range(512, 640)))
    ch.append(list(range(640, 768)))
    for c in range(4):
        ch.append(heads(768, c))
    for n in range(8):
        ch.append(list(range(1280 + n * 128, 1280 + n * 128 + 128)))
        ch.append(list(range(2304 + n * 128, 2304 + n * 128 + 128)))
    bg = [-1] * 128
    for i in range(4):
        bg[i] = 5376 + i
        bg[32 + i] = 5380 + i
    ch.append(bg)
    for h in range(4):
        for base in (3328, 3840, 4352, 4864):
            ch.append(list(range(base + h * 128, base + h * 128 + 128)))
    assert len(ch) == NCH_IN
    return ch


def _yt_rows():
    rows = []
    for kk in range(16):
        if kk < 4:
            rows += list(range(kk * 64, kk * 64 + 64)) + list(range((4 + kk) * 64, (4 + kk) * 64 + 64))
        elif kk < 12:
            rows += list(range(512 + (kk - 4) * 128, 512 + (kk - 4) * 128 + 128))
        else:
            rows += list(range(1536 + (kk - 12) * 128, 1536 + (kk - 12) * 128 + 128))
    return np.array(rows)


def prep_shared(inp, SEQ, NL=2):
    f = np.float32
    w_in = np.asarray(inp["w_in"], f); w_out = np.asarray(inp["w_out"], f)
    wall = np.zeros((NL, NCH, 128, 2048), f)
    chs = _chunk_cols()
    yr = _yt_rows()
    for l in range(NL):
        wz = np.concatenate([w_in[l], np.zeros((2048, 1), f)], axis=1)
        for j, cols in enumerate(chs):
            wc = wz[:, np.array(cols)]
            wall[l, j] = wc.reshape(16, 128, 128).transpose(1, 0, 2).reshape(128, 2048)
        wo = w_out[l][yr]
        for m in range(16):
            wall[l, NCH_IN + m] = wo[:, m * 128:(m + 1) * 128].reshape(16, 128, 128).transpose(1, 0, 2).reshape(128, 2048)
    inv = (1.0 / (10000.0 ** (np.arange(0, 64, 2, dtype=f) / f(64)))).astype(f)
    ang = (np.arange(SEQ, dtype=f)[:, None] * inv[None, :]).astype(f)
    cs, sn = np.cos(ang).astype(f), np.sin(ang).astype(f)
    cosT = np.zeros((128, SEQ), f); sinT = np.zeros((128, SEQ), f)
    for p in range(128):
        d = p % 64
        cosT[p] = cs[:, d % 32]
        sinT[p] = -sn[:, d % 32] if d < 32 else sn[:, d % 32]
    wab = np.stack([np.stack([np.asarray(inp["r_wa"], f)[l].transpose(1, 0, 2).reshape(128, 1024),
                              np.asarray(inp["r_wx"], f)[l].transpose(1, 0, 2).reshape(128, 1024)]) for l in range(NL)])
    cols = np.zeros((128, NL * NCOL_L), f)
    sm4 = np.zeros((4, NL * 2), f)
    for l in range(NL):
        cb = l * NCOL_L
        rcw = np.asarray(inp["r_conv_w"], f)[l]
        for n in range(8):
            for k in range(4):
                cols[:, cb + C_RCW + 4 * n + k] = rcw[k, n * 128:(n + 1) * 128]
            cols[:, cb + C_RCB + n] = np.asarray(inp["r_conv_b"], f)[l, n * 128:(n + 1) * 128]
            cols[:, cb + C_BA + n] = np.asarray(inp["r_ba"], f)[l, n * 128:(n + 1) * 128]
            cols[:, cb + C_BX + n] = np.asarray(inp["r_bx"], f)[l, n * 128:(n + 1) * 128]
            cols[:, cb + C_LAM + n] = np.asarray(inp["r_lam"], f)[l, n * 128:(n + 1) * 128]
        gcw = np.asarray(inp["g_conv_w"], f)[l]
        for ci in range(12):
            for k in range(4):
                cols[:, cb + C_GCW + 4 * ci + k] = gcw[k, ci * 128:(ci + 1) * 128]
        for m in range(16):
            cols[:, cb + C_LNG + m] = np.asarray(inp["ln_g"], f)[l, m * 128:(m + 1) * 128]
            cols[:, cb + C_LNB + m] = np.asarray(inp["ln_b"], f)[l, m * 128:(m + 1) * 128]
        cols[:, cb + C_GNW] = np.asarray(inp["g_norm_w"], f)[l]
        sk = np.asarray(inp["sinks"], f)[l]
        for c in range(4):
            cols[:64, cb + C_SINK + c] = sk[c]
            cols[64:, cb + C_SINK + c] = sk[4 + c]
        sm4[:, 2 * l] = np.asarray(inp["g_a_log"], f)[l]
        sm4[:, 2 * l + 1] = np.asarray(inp["g_dt_bias"], f)[l]
    cst = np.zeros((128, 9 * 128), f)
    I = np.eye(128, dtype=f)
    cst[:, 0:128] = I
    perm = np.zeros((128, 128), f)
    for m in range(128):
        d = m % 64
        perm[(m + 32) if d < 32 else (m - 32), m] = 1.0
    cst[:, 128:256] = perm
    cst[:, 256:384] = 1.0
    cst[:, 384:512] = I - 1.0
    ls = np.zeros((128, 128), f); ls[63, :64] = 1.0; ls[127, 64:] = 1.0
    cst[:, 512:640] = ls
    jj, ii = np.meshgrid(np.arange(128), np.arange(128), indexing="ij")
    cst[:, 640:768] = (jj > ii).astype(f)
    cst[:, 768:896] = (jj <= ii).astype(f)
    cst[:, 896:1024] = np.where((jj // 64 == ii // 64) & (ii >= jj), 0.0, -30000.0).astype(f)
    cst4 = np.zeros((4, 1024), f)
    for h in range(4):
        cst4[h, h * 128:(h + 1) * 128] = 1.0
    rst = np.ones(512, f); rst[::64] = 0.0
    cst4[:, 512:] = rst[None, :]
    return dict(wall=wall, cosT=cosT, sinT=sinT, wab=wab, cols=cols, sm4=sm4, cst=cst, cst4=cst4)


_CACHE = {}


def run_module(inp, SEQ, NL=2, en=(1, 1, 1), trace=False):
    x = np.asarray(inp["x"], np.float32)
    B = x.shape[0]
    shared = prep_shared(inp, SEQ, NL)
    in_maps = []
    for b in range(B):
        m = dict(shared)
        m["xT"] = np.ascontiguousarray(x[b].T).reshape(16, 128, SEQ)
        in_maps.append(m)
    key = (SEQ, NL, tuple(en))
    nc, P, st = build_program(SEQ, NL, en)
    P.schedule(WINDOW)
    P.emit()
    st.close()
    res = run_bass_kernel_spmd(nc, in_maps, core_ids=list(range(B)))
    out = np.stack([np.ascontiguousarray(res.results[b]["outT"].reshape(2048, SEQ).T) for b in range(B)])
    return out.astype(np.float32)


def kernel(**inputs):
    return run_module(inputs, 8192, 2, (1, 1, 1))
```
